# Optimizing a Trainium2 kernel written in Bass

```python
import math
import jax, jax.numpy as jnp
from jax import lax
import numpy as np

D_MODEL = 1024
BATCH = 4
SEQ = 8192
DEPTH = 4

GRID_W = 64
CTX_LEN = 256
N_MIXERS = 3
N_POOL_LAYERS = len(range(0, DEPTH, N_MIXERS))
N_DIFF_LAYERS = len(range(1, DEPTH, N_MIXERS))
N_NAT_LAYERS = len(range(2, DEPTH, N_MIXERS))
D_FF = ((8 * D_MODEL // 3 + 255) // 256) * 256
N_MOD = 9
POOL_WINDOWS = (2, 4, 8, 16)
POOL_GROUPS = len(POOL_WINDOWS)
POOL_GW = D_MODEL // POOL_GROUPS
DIFF_HEADS = 8
DIFF_HD = D_MODEL // DIFF_HEADS // 2
NAT_HEADS = 16
NAT_HD = D_MODEL // NAT_HEADS
NAT_WIN_ROWS = 8
NAT_WIN_COLS = 16
ROPE_THETA = 10000.0
Q_BLOCK = 128
NORM_EPS = 1e-6

kernel_name = 'hybrid_pool_diffattn_natten_macaron_dit'


def rms_norm(x, g):
    xf = x.astype(jnp.float32)
    y = xf * lax.rsqrt(jnp.mean(xf * xf, axis=-1, keepdims=True) + NORM_EPS)
    return (y * g.astype(jnp.float32)).astype(x.dtype)


def modulate(h, shift, scale):
    return h * (1 + scale) + shift


def swiglu(h, w_in, w_out):
    g, u = jnp.split(h @ w_in, 2, axis=-1)
    return (jax.nn.silu(g) * u) @ w_out


def ffn_step(s, g, shift, scale, gate, w_in, w_out):
    return s + 0.5 * gate * swiglu(modulate(rms_norm(s, g), shift, scale), w_in, w_out)


def pool_mix(h, w, scale):
    b, n, d = h.shape
    hf = h.astype(jnp.float32)
    cs = jnp.concatenate([jnp.zeros((b, 1, d), jnp.float32), jnp.cumsum(hf, axis=1)], axis=1)
    t = jnp.arange(n)
    outs = []
    for gi, win in enumerate(POOL_WINDOWS):
        lo = jnp.clip(t - win // 2, 0, n - 1)
        hi = jnp.clip(t - win // 2 + win - 1, 0, n - 1)
        sl = slice(gi * POOL_GW, (gi + 1) * POOL_GW)
        csg = cs[:, :, sl]
        mean = (csg[:, hi + 1] - csg[:, lo]) / (hi - lo + 1).astype(jnp.float32)[None, :, None]
        outs.append(mean - hf[:, :, sl])
    diff = jnp.stack(outs, axis=2).astype(h.dtype)
    y = jnp.einsum('bngc,gce->bnge', diff, w).reshape(b, n, d)
    return y * scale


def axial_angles(n):
    t = jnp.arange(n)
    rows = (t // GRID_W).astype(jnp.float32)
    cols = (t % GRID_W).astype(jnp.float32)
    per_axis = DIFF_HD // 2
    inv = ROPE_THETA ** (-jnp.arange(0, per_axis, 2, dtype=jnp.float32) / per_axis)
    return rows[:, None] * inv, cols[:, None] * inv


def rotate(x, ang):
    m = x.shape[-1] // 2
    x1, x2 = x[..., :m], x[..., m:]
    cos, sin = jnp.cos(ang), jnp.sin(ang)
    return jnp.concatenate([x1 * cos - x2 * sin, x2 * cos + x1 * sin], axis=-1)


def rope_2d(x, row_ang, col_ang):
    shape = (1, x.shape[1]) + (1,) * (x.ndim - 3) + (row_ang.shape[-1],)
    ra, ca = row_ang.reshape(shape), col_ang.reshape(shape)
    xf = x.astype(jnp.float32)
    half = x.shape[-1] // 2
    out = jnp.concatenate([rotate(xf[..., :half], ra), rotate(xf[..., half:], ca)], axis=-1)
    return out.astype(x.dtype)


def diff_attention(h_lat, h_ctx, w_qkv, lam, subln_g, w_o, lam_init, want_ctx):
    b, n, d = h_lat.shape
    scale = DIFF_HD ** -0.5

    def proj(h):
        q, k, v = jnp.split(h @ w_qkv, 3, axis=-1)
        s = h.shape[:2]
        return (q.reshape(s + (DIFF_HEADS, 2, DIFF_HD)), k.reshape(s + (DIFF_HEADS, 2, DIFF_HD)),
                v.reshape(s + (DIFF_HEADS, 2 * DIFF_HD)))

    q_l, k_l, v_l = proj(h_lat)
    q_c, k_c, v_c = proj(h_ctx)
    ra, ca = axial_angles(n)
    q_l = rope_2d(q_l, ra, ca)
    k_l = rope_2d(k_l, ra, ca)
    lf = lam.astype(jnp.float32)
    lam_full = jnp.exp(jnp.sum(lf[0] * lf[1])) - jnp.exp(jnp.sum(lf[2] * lf[3])) + lam_init

    def attend(q, k, v):
        s = jnp.einsum('bqhjd,bkhjd->bhjqk', q, k).astype(jnp.float32) * scale
        p = jax.nn.softmax(s, axis=-1)
        a = p[:, :, 0] - lam_full * p[:, :, 1]
        return jnp.einsum('bhqk,bkhe->bqhe', a.astype(v.dtype), v)

    k_all = jnp.concatenate([k_l, k_c], axis=1)
    v_all = jnp.concatenate([v_l, v_c], axis=1)
    nb = n // Q_BLOCK
    q_blocks = jnp.moveaxis(q_l.reshape(b, nb, Q_BLOCK, DIFF_HEADS, 2, DIFF_HD), 1, 0)
    o_l = lax.map(lambda qb: attend(qb, k_all, v_all), q_blocks)
    o_l = jnp.moveaxis(o_l, 0, 1).reshape(b, n, DIFF_HEADS, 2 * DIFF_HD)

    def finish(o):
        o = rms_norm(o, subln_g) * (1 - lam_init)
        return o.reshape(o.shape[:2] + (d,)) @ w_o

    out_l = finish(o_l)
    out_c = finish(attend(q_c, k_c, v_c)) if want_ctx else None
    return out_l, out_c


def neighbourhood_attention(h_lat, h_ctx, w_qkv, b_qkv, rpb, w_o, b_o, want_ctx):
    b, n, d = h_lat.shape
    n_rows = n // GRID_W
    wr = min(NAT_WIN_ROWS, n_rows)
    wc = min(NAT_WIN_COLS, GRID_W)
    scale = NAT_HD ** -0.5

    def proj(h):
        q, k, v = jnp.split(h @ w_qkv + b_qkv, 3, axis=-1)
        s = h.shape[:2] + (NAT_HEADS, NAT_HD)
        return q.reshape(s) * scale, k.reshape(s), v.reshape(s)

    q_l, k_l, v_l = proj(h_lat)
    q_c, k_c, v_c = proj(h_ctx)
    gshape = (b, n_rows, GRID_W, NAT_HEADS, NAT_HD)
    k_g = k_l.reshape(gshape)
    v_g = v_l.reshape(gshape)
    q_rows = jnp.moveaxis(q_l.reshape(gshape), 1, 0)
    qcol = jnp.arange(GRID_W)
    col_start = jnp.clip(qcol - wc // 2, 0, GRID_W - wc)
    col_idx = col_start[:, None] + jnp.arange(wc)[None, :]
    col_off = col_idx - qcol[:, None] + (NAT_WIN_COLS - 1)

    def row_block(args):
        r, q = args
        rs = jnp.clip(r - wr // 2, 0, n_rows - wr)
        k_band = lax.dynamic_slice_in_dim(k_g, rs, wr, axis=1)
        v_band = lax.dynamic_slice_in_dim(v_g, rs, wr, axis=1)
        k_win = k_band[:, :, col_idx]
        v_win = v_band[:, :, col_idx]
        row_off = rs + jnp.arange(wr) - r + (NAT_WIN_ROWS - 1)
        bias = rpb[:, row_off[:, None, None], col_off[None, :, :]]
        bias = jnp.transpose(bias, (0, 2, 1, 3)).astype(jnp.float32)
        s_win = jnp.einsum('bqhd,bwqchd->bhqwc', q, k_win).astype(jnp.float32) + bias[None]
        s_ctx = jnp.einsum('bqhd,bkhd->bhqk', q, k_c).astype(jnp.float32)
        s = jnp.concatenate([s_win.reshape(b, NAT_HEADS, GRID_W, wr * wc), s_ctx], axis=-1)
        p = jax.nn.softmax(s, axis=-1).astype(v_win.dtype)
        p_win = p[..., :wr * wc].reshape(b, NAT_HEADS, GRID_W, wr, wc)
        p_ctx = p[..., wr * wc:]
        return (jnp.einsum('bhqwc,bwqchd->bqhd', p_win, v_win)
                + jnp.einsum('bhqk,bkhd->bqhd', p_ctx, v_c))

    o = lax.map(row_block, (jnp.arange(n_rows), q_rows))
    out_l = jnp.moveaxis(o, 0, 1).reshape(b, n, d) @ w_o + b_o
    out_c = None
    if want_ctx:
        s = jnp.einsum('bqhd,bkhd->bhqk', q_c, k_c).astype(jnp.float32)
        p = jax.nn.softmax(s, axis=-1).astype(v_c.dtype)
        o_c = jnp.einsum('bhqk,bkhd->bqhd', p, v_c)
        out_c = o_c.reshape(o_c.shape[0], o_c.shape[1], d) @ w_o + b_o
    return out_l, out_c


def setup_inputs(seed: int = 0) -> dict:
    key = jax.random.key(seed)
    ks = jax.random.split(key, 24)
    D = D_MODEL

    def nrm(k, shape, std):
        return jax.random.normal(k, shape, jnp.float32) * std

    return {
        'x': nrm(ks[0], (BATCH, SEQ, D), 1.0),
        'c': nrm(ks[1], (BATCH, D), 1.0),
        'ctx': nrm(ks[2], (BATCH, CTX_LEN, D), 1.0),
        'c_ctx': nrm(ks[3], (D,), 1.0),
        'ada_w': nrm(ks[4], (DEPTH, D, N_MOD * D), 0.5 * D ** -0.5),
        'ada_b': nrm(ks[5], (DEPTH, N_MOD * D), 0.01),
        'norm_g': 1.0 + nrm(ks[6], (DEPTH, 3, D), 0.1),
        'ffn_w_in': nrm(ks[7], (DEPTH, 2, D, 2 * D_FF), D ** -0.5),
        'ffn_w_out': nrm(ks[8], (DEPTH, 2, D_FF, D), D_FF ** -0.5),
        'pool_w': nrm(ks[9], (N_POOL_LAYERS, POOL_GROUPS, POOL_GW, POOL_GW), POOL_GW ** -0.5),
        'pool_scale': 1.0 + nrm(ks[10], (N_POOL_LAYERS, D), 0.1),
        'diff_w_qkv': nrm(ks[11], (N_DIFF_LAYERS, D, 3 * D), D ** -0.5),
        'diff_lam': nrm(ks[12], (N_DIFF_LAYERS, 4, DIFF_HD), 0.1),
        'diff_subln_g': 1.0 + nrm(ks[13], (N_DIFF_LAYERS, 2 * DIFF_HD), 0.1),
        'diff_w_o': nrm(ks[14], (N_DIFF_LAYERS, D, D), D ** -0.5),
        'nat_w_qkv': nrm(ks[15], (N_NAT_LAYERS, D, 3 * D), D ** -0.5),
        'nat_b_qkv': nrm(ks[16], (N_NAT_LAYERS, 3 * D), 0.01),
        'nat_rpb': nrm(ks[17], (N_NAT_LAYERS, NAT_HEADS, 2 * NAT_WIN_ROWS - 1, 2 * NAT_WIN_COLS - 1), 0.1),
        'nat_w_o': nrm(ks[18], (N_NAT_LAYERS, D, D), D ** -0.5),
        'nat_b_o': nrm(ks[19], (N_NAT_LAYERS, D), 0.01),
        'final_g': 1.0 + nrm(ks[20], (D,), 0.1),
    }


def reference(x, c, ctx, c_ctx, ada_w, ada_b, norm_g, ffn_w_in, ffn_w_out, pool_w, pool_scale,
              diff_w_qkv, diff_lam, diff_subln_g, diff_w_o, nat_w_qkv, nat_b_qkv, nat_rpb,
              nat_w_o, nat_b_o, final_g):
    xc = ctx
    s_c = jax.nn.silu(c)
    s_cc = jax.nn.silu(c_ctx)
    for i in range(DEPTH):
        kind = i % N_MIXERS
        j = i // N_MIXERS
        last = i == DEPTH - 1
        update_ctx = not last
        ctx_needed = update_ctx or kind != 0
        mx = jnp.split((s_c @ ada_w[i] + ada_b[i])[:, None, :], N_MOD, axis=-1)
        mc = jnp.split((s_cc @ ada_w[i] + ada_b[i])[None, None, :], N_MOD, axis=-1)
        x = ffn_step(x, norm_g[i, 0], mx[0], mx[1], mx[2], ffn_w_in[i, 0], ffn_w_out[i, 0])
        if ctx_needed:
            xc = ffn_step(xc, norm_g[i, 0], mc[0], mc[1], mc[2], ffn_w_in[i, 0], ffn_w_out[i, 0])
        hx = modulate(rms_norm(x, norm_g[i, 1]), mx[3], mx[4])
        hc = modulate(rms_norm(xc, norm_g[i, 1]), mc[3], mc[4]) if ctx_needed else None
        if kind == 0:
            ox = pool_mix(hx, pool_w[j], pool_scale[j])
            oc = pool_mix(hc, pool_w[j], pool_scale[j]) if update_ctx else None
        elif kind == 1:
            lam_init = 0.8 - 0.6 * math.exp(-0.3 * i)
            ox, oc = diff_attention(hx, hc, diff_w_qkv[j], diff_lam[j], diff_subln_g[j], diff_w_o[j],
                                    lam_init, update_ctx)
        else:
            ox, oc = neighbourhood_attention(hx, hc, nat_w_qkv[j], nat_b_qkv[j], nat_rpb[j],
                                             nat_w_o[j], nat_b_o[j], update_ctx)
        x = x + mx[5] * ox
        if update_ctx:
            xc = xc + mc[5] * oc
        x = ffn_step(x, norm_g[i, 2], mx[6], mx[7], mx[8], ffn_w_in[i, 1], ffn_w_out[i, 1])
        if update_ctx:
            xc = ffn_step(xc, norm_g[i, 2], mc[6], mc[7], mc[8], ffn_w_in[i, 1], ffn_w_out[i, 1])
    return rms_norm(x, final_g)
```

```python
import math
from contextlib import ExitStack

import numpy as np
import concourse.bass as bass
import concourse.mybir as mybir
from concourse.bass_utils import run_bass_kernel_spmd

F32 = mybir.dt.float32
BF16 = mybir.dt.bfloat16
AF = mybir.ActivationFunctionType
ALU = mybir.AluOpType

D = 1024
KC = 8
DFF = 2816
FC = 22
NLAT = 4096
NCTX = 256
NTOK = NLAT + NCTX
DEPTH = 4
EPS = 1e-6
NCORES = 8


class Buf:
    __slots__ = ("name", "w", "r")

    def __init__(self, name=""):
        self.name = name
        self.w = None
        self.r = []


class Op:
    __slots__ = ("eng", "fn", "deps", "dma", "sig", "tok", "cc")

    def __init__(self, eng, fn, deps, dma):
        self.eng, self.fn, self.deps, self.dma = eng, fn, deps, dma
        self.sig = False
        self.tok = None
        self.cc = False


STRICT_ALL = False
COMPUTE = ("vector", "scalar", "gpsimd", "tensor")
QUEUES = ("sync", "scalar", "vector", "gpsimd", "tensor")


class Prog:
    def __init__(self, nc, es, n_dma_sems=40):
        self.nc = nc
        self.ops = []
        self.esem = {e: es.enter_context(nc.semaphore("e_" + e)) for e in COMPUTE}
        self.dsem = [es.enter_context(nc.semaphore("d%d" % i)) for i in range(n_dma_sems)]
        self.ccsem = [es.enter_context(nc.semaphore("cc%d" % i)) for i in range(24)]
        self.ccnext = 0
        self.ecnt = {e: 0 for e in COMPUTE}
        self.dcnt = [0] * n_dma_sems
        self.dnext = 0
        self.waited = {q: {} for q in QUEUES}
        self.done_upto = 0

    def op(self, eng, fn, reads=(), writes=(), dma=False, strict=False):
        raw = set()
        deps = set()
        for b in reads:
            if b.w is not None:
                raw.add(b.w)
        for b in writes:
            if b.w is not None:
                deps.add(b.w)
            deps.update(b.r)
        oid = len(self.ops)
        real = []
        for d in raw | deps:
            if d < self.done_upto:
                continue
            o = self.ops[d]
            if (not dma) and (not o.dma) and o.eng == eng:
                if eng == "tensor" or d not in raw:
                    continue
            o.sig = True
            real.append(d)
        self.ops.append(Op(eng, fn, real, dma))
        if dma:
            self.ops[oid].sig = True
        for b in reads:
            b.r.append(oid)
        for b in writes:
            b.w = oid
            b.r = []
        return oid

    def cc(self, fn, reads=(), writes=()):
        oid = self.op("gpsimd", fn, reads=reads, writes=writes, dma=True)
        self.ops[oid].cc = True
        return oid

    def flush(self, final_wait_eng="sync"):
        nc = self.nc
        start = self.done_upto
        ops = self.ops
        per_eng = {q: [] for q in QUEUES}
        plans = []
        for i in range(start, len(ops)):
            o = ops[i]
            waits = []
            w = self.waited[o.eng]
            for d in o.deps:
                sem, val = ops[d].tok
                if w.get(id(sem), (None, 0))[1] >= val:
                    continue
                w[id(sem)] = (sem, val)
                waits.append((sem, val))
            inc = None
            if o.cc:
                sem = self.ccsem[self.ccnext]
                self.ccnext += 1
                o.tok = (sem, 1)
                inc = (sem, None)
            elif o.dma:
                k = self.dnext
                self.dnext = (self.dnext + 1) % len(self.dsem)
                sem = self.dsem[k]
                if self.dcnt[k] > 0 and w.get(id(sem), (None, 0))[1] < self.dcnt[k]:
                    waits.append((sem, self.dcnt[k]))
                    w[id(sem)] = (sem, self.dcnt[k])
                self.dcnt[k] += 16
                o.tok = (sem, self.dcnt[k])
                inc = (sem, 16)
            elif o.sig:
                sem = self.esem[o.eng]
                self.ecnt[o.eng] += 1
                o.tok = (sem, self.ecnt[o.eng])
                inc = (sem, 1)
            per_eng[o.eng].append((o.fn, waits, inc))
        self.done_upto = len(ops)
        finals = [(self.dsem[k], self.dcnt[k]) for k in range(len(self.dsem)) if self.dcnt[k] > 0]
        finals += [(self.ccsem[k], 1) for k in range(self.ccnext)]

        def mk(q, lst, fin):
            def body(e):
                for fn, waits, inc in lst:
                    for sem, val in waits:
                        e.wait_ge(sem, val)
                    ins = fn(e)
                    if inc is not None:
                        if inc[1] is None:
                            ins.then_inc(inc[0])
                        else:
                            ins.then_inc(inc[0], inc[1])
                for sem, val in fin:
                    e.wait_ge(sem, val)
            return body

        with nc.Block() as block:
            for q in QUEUES:
                fin = finals if q == final_wait_eng else []
                if per_eng[q] or fin:
                    getattr(block, q)(mk(q, per_eng[q], fin))
        for q in QUEUES:
            self.waited[q] = dict(self.waited[q])


_uid = [0]


def sb(nc, es, name, shape, dt):
    _uid[0] += 1
    return es.enter_context(nc.sbuf_tensor("%s_%d" % (name, _uid[0]), shape, dt))


def pp(nc, es, name, shape, dt):
    _uid[0] += 1
    return es.enter_context(nc.psum_tensor("%s_%d" % (name, _uid[0]), shape, dt))


def _act(out, in_, func, bias=None, scale=None):
    kw = {}
    if bias is not None:
        kw["bias"] = bias
    if scale is not None:
        kw["scale"] = scale
    return lambda e: e.activation(out=out, in_=in_, func=func, **kw)


def lat_ctx_tiles(nlat=NLAT, nctx=NCTX, tile=512):
    t = [(c0, tile, 0) for c0 in range(0, nlat, tile)]
    if nctx:
        t.append((nlat, nctx, 1))
    return t


def mods_stage(nc, P, cT, ada_w, ada_bT, gT, vec_out):
  with ExitStack() as es:
    NW = 1152
    NB = 3
    sT = sb(nc, es, "m_sT", [128, KC, 2], F32)
    cin = sb(nc, es, "m_cin", [128, KC, 2], F32)
    bT = sb(nc, es, "m_bT", [128, DEPTH, 72], F32)
    gs = sb(nc, es, "m_gs", [128, DEPTH, 3, KC], F32)
    modt = sb(nc, es, "m_mod", [128, 72, 2], F32)
    vec = sb(nc, es, "m_vec", [128, DEPTH, 9, 2, KC], F32)
    wb = [sb(nc, es, "m_w%d" % i, [128, KC, NW], F32) for i in range(NB)]
    ps = [pp(nc, es, "m_ps%d" % i, [128, 72, 2], F32) for i in range(2)]
    b_c, b_s, b_b, b_g, b_mod, b_vec = (Buf() for _ in range(6))
    b_w = [Buf() for _ in range(NB)]
    b_ps = [Buf() for _ in range(2)]

    P.op("sync", lambda e: e.dma_start(out=cin[:], in_=cT), writes=[b_c], dma=True)
    P.op("sync", lambda e: e.dma_start(out=bT[:], in_=ada_bT), writes=[b_b], dma=True)
    P.op("sync", lambda e: e.dma_start(out=gs[:], in_=gT), writes=[b_g], dma=True)
    P.op("scalar", _act(sT[:], cin[:], AF.Silu), reads=[b_c], writes=[b_s])
    wv = ada_w.rearrange("l (k p) n -> l p k n", p=128)
    li = 0
    for l in range(DEPTH):
        pst = ps[l % 2]
        for blk in range(9216 // NW):
            slot = li % NB
            q = "sync" if li % 2 == 0 else "gpsimd"
            P.op(q, (lambda e, l=l, blk=blk, slot=slot: e.dma_start(
                out=wb[slot][:], in_=wv[l, :, :, blk * NW:(blk + 1) * NW])),
                writes=[b_w[slot]], dma=True)
            li += 1
            for nn in range(NW // 128):
                n = blk * (NW // 128) + nn
                for k in range(KC):
                    P.op("tensor", (lambda e, pst=pst, n=n, slot=slot, nn=nn, k=k: e.matmul(
                        pst[:, n, :], lhsT=wb[slot][:, k, nn * 128:(nn + 1) * 128], rhs=sT[:, k, :],
                        start=(k == 0), stop=(k == KC - 1))),
                        reads=[b_w[slot], b_s], writes=[b_ps[l % 2]])
        for s in range(2):
            P.op("vector", (lambda e, pst=pst, l=l, s=s: e.tensor_tensor(
                out=modt[:, :, s], in0=pst[:, :, s], in1=bT[:, l, :], op=ALU.add)),
                reads=[b_ps[l % 2], b_b], writes=[b_mod])
        for u in range(3):
            for s in range(2):
                P.op("vector", (lambda e, l=l, u=u, s=s: e.scalar_tensor_tensor(
                    out=vec[:, l, 3 * u, s, :], in0=modt[:, (3 * u + 1) * 8:(3 * u + 2) * 8, s], scalar=1.0,
                    in1=gs[:, l, u, :], op0=ALU.add, op1=ALU.mult)),
                    reads=[b_mod, b_g], writes=[b_vec])
                P.op("vector", (lambda e, l=l, u=u, s=s: e.tensor_copy(
                    out=vec[:, l, 3 * u + 1, s, :], in_=modt[:, (3 * u) * 8:(3 * u + 1) * 8, s])),
                    reads=[b_mod], writes=[b_vec])
                P.op("vector", (lambda e, l=l, u=u, s=s: e.tensor_scalar(
                    out=vec[:, l, 3 * u + 2, s, :], in0=modt[:, (3 * u + 2) * 8:(3 * u + 3) * 8, s],
                    scalar1=(1.0 if u == 1 else 0.5), scalar2=None, op0=ALU.mult)),
                    reads=[b_mod], writes=[b_vec])
    P.op("sync", lambda e: e.dma_start(out=vec_out, in_=vec[:]), reads=[b_vec], dma=True)
    P.flush()


class Consts:
    def __init__(self, nc, P, es):
        self.ones_bf = sb(nc, es, "c_ones", [128, 128], BF16)
        self.b_ones = Buf()
        P.op("vector", lambda e: e.memset(self.ones_bf[:], 1.0), writes=[self.b_ones])


def emit_norm_mod(nc, P, C, xt, b_x, n, sq, b_sq, ss_ps, b_ss, rstd, b_rstd, xr, b_xr, hb, b_hb,
                  A, Bv, b_vec, mod_eng="gpsimd"):
    P.op("scalar", _act(sq[:, :, :n], xt[:, :, :n], AF.Square), reads=[b_x], writes=[b_sq])
    for k in range(KC):
        P.op("tensor", (lambda e, k=k: e.matmul(ss_ps[:, :n], lhsT=C.ones_bf[:], rhs=sq[:, k, :n],
                                               start=(k == 0), stop=(k == KC - 1))),
             reads=[b_sq, C.b_ones], writes=[b_ss])
    P.op("scalar", _act(rstd[:, :n], ss_ps[:, :n], AF.Sqrt, bias=EPS, scale=1.0 / D),
         reads=[b_ss], writes=[b_rstd])
    P.op("vector", lambda e: e.reciprocal(out=rstd[:, :n], in_=rstd[:, :n]), reads=[b_rstd], writes=[b_rstd])
    for k in range(KC):
        j = k % 2
        P.op("vector", (lambda e, k=k, j=j: e.tensor_tensor(out=xr[j][:, :n], in0=xt[:, k, :n], in1=rstd[:, :n],
                                                         op=ALU.mult)),
             reads=[b_x, b_rstd], writes=[b_xr[j]])
        if mod_eng == "scalar":
            P.op("scalar", _act(hb[:, k, :n], xr[j][:, :n], AF.Identity, bias=Bv[:, k:k + 1], scale=A[:, k:k + 1]),
                 reads=[b_xr[j], b_vec], writes=[b_hb])
        else:
            P.op("gpsimd", (lambda e, k=k, j=j: e.tensor_scalar(out=hb[:, k, :n], in0=xr[j][:, :n],
                                                             scalar1=A[:, k:k + 1], scalar2=Bv[:, k:k + 1],
                                                             op0=ALU.mult, op1=ALU.add)),
                 reads=[b_xr[j], b_vec], writes=[b_hb])


def ffn_stage(nc, P, xin, xout, w_in, w_out, vec_d, layer, u, tiles, final_g=None):
    with ExitStack() as es:
        C = Consts(nc, P, es)
        NT = 512
        wi = sb(nc, es, "f_wi", [128, KC, 2 * DFF], BF16)
        wo = sb(nc, es, "f_wo", [128, FC, D], BF16)
        xt = [sb(nc, es, "f_xt%d" % i, [128, KC, NT], F32) for i in range(2)]
        hb = sb(nc, es, "f_hb", [128, KC, NT], BF16)
        ab = sb(nc, es, "f_ab", [128, FC, NT], BF16)
        rstd = sb(nc, es, "f_rstd", [128, NT], F32)
        xr = [sb(nc, es, "f_xr%d" % i, [128, NT], F32) for i in range(2)]
        sg = [sb(nc, es, "f_sg%d" % i, [128, NT], F32) for i in range(2)]
        vec = sb(nc, es, "f_vec", [128, 3, 2, KC], F32)
        gps = [pp(nc, es, "f_g%d" % i, [128, NT], F32) for i in range(2)]
        ups = [pp(nc, es, "f_u%d" % i, [128, NT], F32) for i in range(2)]
        yps = [pp(nc, es, "f_y%d" % i, [128, NT], F32) for i in range(2)]
        ssp = pp(nc, es, "f_ss", [128, NT], F32)
        b_xt = [Buf(), Buf()]
        b_hb, b_rstd, b_vec, b_ss = Buf(), Buf(), Buf(), Buf()
        b_ab = [Buf() for _ in range(FC)]
        b_xr = [Buf(), Buf()]
        b_sg = [Buf(), Buf()]
        b_g = [Buf(), Buf()]
        b_u = [Buf(), Buf()]
        b_y = [Buf(), Buf()]

        P.op("sync", lambda e: e.dma_start(out=vec[:], in_=vec_d[:, layer, 3 * u:3 * u + 3, :, :]),
             writes=[b_vec], dma=True)
        if final_g is not None:
            gf = sb(nc, es, "f_gf", [128, KC], F32)
            b_gf = Buf()
            P.op("sync", lambda e: e.dma_start(out=gf[:], in_=final_g), writes=[b_gf], dma=True)
        wiv = w_in.rearrange("(k p) n -> p k n", p=128)
        wov = w_out.rearrange("(f p) n -> p f n", p=128)
        CB = 512
        nblk = (DFF + CB - 1) // CB
        b_wi = {}
        b_wo = {}
        for blk in range(nblk):
            c0 = blk * CB
            c1 = min(DFF, c0 + CB)
            for half in range(2):
                bb = Buf()
                b_wi[(half, blk)] = bb
                o = half * DFF
                P.op("gpsimd", (lambda e, o=o, c0=c0, c1=c1: e.dma_start(
                    out=wi[:, :, o + c0:o + c1], in_=wiv[:, :, o + c0:o + c1])), writes=[bb], dma=True)
        for fg in range(0, FC, 4):
            f1 = min(FC, fg + 4)
            bb = Buf()
            for f in range(fg, f1):
                b_wo[f] = bb
            P.op("gpsimd", (lambda e, fg=fg, f1=f1: e.dma_start(out=wo[:, fg:f1, :], in_=wov[:, fg:f1, :])),
                 writes=[bb], dma=True)

        xiv = xin.rearrange("(k p) t -> p k t", p=128)
        xov = xout.rearrange("(k p) t -> p k t", p=128)

        def load(ti):
            c0, n, s = tiles[ti]
            slot = ti % 2
            P.op("sync", (lambda e: e.dma_start(out=xt[slot][:, :, :n], in_=xiv[:, :, c0:c0 + n])),
                 writes=[b_xt[slot]], dma=True)

        def norm(ti):
            c0, n, s = tiles[ti]
            slot = ti % 2
            emit_norm_mod(nc, P, C, xt[slot], b_xt[slot], n, hb, b_hb, ssp, b_ss, rstd, b_rstd, xr, b_xr,
                          hb, b_hb, vec[:, 0, s, :], vec[:, 1, s, :], b_vec)

        def phase1(ti):
            c0, n, s = tiles[ti]
            for f in range(FC):
                j = f % 2
                blk = (f * 128) // CB
                for half, pst, bp in ((0, gps[j], b_g[j]), (1, ups[j], b_u[j])):
                    o = half * DFF + f * 128
                    for k in range(KC):
                        P.op("tensor", (lambda e, pst=pst, o=o, k=k: e.matmul(
                            pst[:, :n], lhsT=wi[:, k, o:o + 128], rhs=hb[:, k, :n],
                            start=(k == 0), stop=(k == KC - 1))),
                            reads=[b_wi[(half, blk)], b_hb], writes=[bp])
                P.op("scalar", _act(sg[j][:, :n], gps[j][:, :n], AF.Silu), reads=[b_g[j]], writes=[b_sg[j]])
                P.op("vector", (lambda e, j=j, f=f: e.tensor_tensor(out=ab[:, f, :n], in0=ups[j][:, :n],
                                                                 in1=sg[j][:, :n], op=ALU.mult)),
                     reads=[b_u[j], b_sg[j]], writes=[b_ab[f]])

        def phase2(ti):
            c0, n, s = tiles[ti]
            slot = ti % 2
            for d in range(KC):
                j = d % 2
                for f in range(FC):
                    P.op("tensor", (lambda e, j=j, f=f, d=d: e.matmul(
                        yps[j][:, :n], lhsT=wo[:, f, d * 128:(d + 1) * 128], rhs=ab[:, f, :n],
                        start=(f == 0), stop=(f == FC - 1))),
                        reads=[b_wo[f], b_ab[f]], writes=[b_y[j]])
                P.op("vector", (lambda e, j=j, d=d: e.scalar_tensor_tensor(
                    out=xt[slot][:, d, :n], in0=yps[j][:, :n], scalar=vec[:, 2, s, d:d + 1],
                    in1=xt[slot][:, d, :n], op0=ALU.mult, op1=ALU.add)),
                    reads=[b_y[j], b_vec, b_xt[slot]], writes=[b_xt[slot]])
            if final_g is not None:
                x = xt[slot]
                P.op("scalar", _act(ab[:, 0:KC, :n], x[:, :, :n], AF.Square), reads=[b_xt[slot]], writes=b_ab[0:KC])
                for k in range(KC):
                    P.op("tensor", (lambda e, k=k: e.matmul(ssp[:, :n], lhsT=C.ones_bf[:], rhs=ab[:, k, :n],
                                                            start=(k == 0), stop=(k == KC - 1))),
                         reads=[b_ab[k], C.b_ones], writes=[b_ss])
                P.op("scalar", _act(rstd[:, :n], ssp[:, :n], AF.Sqrt, bias=EPS, scale=1.0 / D),
                     reads=[b_ss], writes=[b_rstd])
                P.op("vector", (lambda e: e.reciprocal(out=rstd[:, :n], in_=rstd[:, :n])), reads=[b_rstd], writes=[b_rstd])
                for k in range(KC):
                    P.op("vector", (lambda e, k=k: e.scalar_tensor_tensor(
                        out=x[:, k, :n], in0=x[:, k, :n], scalar=gf[:, k:k + 1], in1=rstd[:, :n],
                        op0=ALU.mult, op1=ALU.mult)),
                        reads=[b_xt[slot], b_rstd, b_gf], writes=[b_xt[slot]])
            P.op("sync", (lambda e: e.dma_start(out=xov[:, :, c0:c0 + n], in_=xt[slot][:, :, :n])),
                 reads=[b_xt[slot]], dma=True)

        nt = len(tiles)
        load(0)
        if nt > 1:
            load(1)
        norm(0)
        for ti in range(nt):
            phase1(ti)
            if ti + 1 < nt:
                norm(ti + 1)
            phase2(ti)
            if ti + 2 < nt:
                load(ti + 2)
        P.flush()


def final_norm_stage(nc, P, xin, xout, gfT, tiles):
    with ExitStack() as es:
        C = Consts(nc, P, es)
        NT = 512
        xt = [sb(nc, es, "n_xt%d" % i, [128, KC, NT], F32) for i in range(2)]
        sq = sb(nc, es, "n_sq", [128, KC, NT], BF16)
        rstd = sb(nc, es, "n_rstd", [128, NT], F32)
        gf = sb(nc, es, "n_gf", [128, KC], F32)
        ssp = pp(nc, es, "n_ss", [128, NT], F32)
        b_xt = [Buf(), Buf()]
        b_sq, b_rstd, b_gf, b_ss = Buf(), Buf(), Buf(), Buf()
        P.op("sync", lambda e: e.dma_start(out=gf[:], in_=gfT), writes=[b_gf], dma=True)
        xiv = xin.rearrange("(k p) t -> p k t", p=128)
        xov = xout.rearrange("(k p) t -> p k t", p=128)
        for ti, (c0, n, s) in enumerate(tiles):
            slot = ti % 2
            x = xt[slot]
            P.op("sync", (lambda e, x=x, c0=c0, n=n: e.dma_start(out=x[:, :, :n], in_=xiv[:, :, c0:c0 + n])),
                 writes=[b_xt[slot]], dma=True)
            P.op("scalar", _act(sq[:, :, :n], x[:, :, :n], AF.Square), reads=[b_xt[slot]], writes=[b_sq])
            for k in range(KC):
                P.op("tensor", (lambda e, k=k, n=n: e.matmul(ssp[:, :n], lhsT=C.ones_bf[:], rhs=sq[:, k, :n],
                                                          start=(k == 0), stop=(k == KC - 1))),
                     reads=[b_sq, C.b_ones], writes=[b_ss])
            P.op("scalar", _act(rstd[:, :n], ssp[:, :n], AF.Sqrt, bias=EPS, scale=1.0 / D),
                 reads=[b_ss], writes=[b_rstd])
            P.op("vector", (lambda e, n=n: e.reciprocal(out=rstd[:, :n], in_=rstd[:, :n])),
                 reads=[b_rstd], writes=[b_rstd])
            for k in range(KC):
                P.op("vector", (lambda e, x=x, k=k, n=n: e.scalar_tensor_tensor(
                    out=x[:, k, :n], in0=x[:, k, :n], scalar=gf[:, k:k + 1], in1=rstd[:, :n],
                    op0=ALU.mult, op1=ALU.mult)),
                    reads=[b_xt[slot], b_rstd, b_gf], writes=[b_xt[slot]])
            P.op("sync", (lambda e, x=x, c0=c0, n=n: e.dma_start(out=xov[:, :, c0:c0 + n], in_=x[:, :, :n])),
                 reads=[b_xt[slot]], dma=True)
        P.flush()


POOL_W = (2, 4, 8, 16)
PT = 496
NPOOL = NLAT + 16 + NCTX + 16


def pool_tiles(with_ctx=True):
    t = []
    c = 0
    while c < NLAT:
        n = min(PT, NLAT - c)
        t.append((c, n, 0, c == 0, c + n == NLAT, c))
        c += n
    if with_ctx:
        t.append((NLAT, NCTX, 1, True, True, NLAT))
    return t


def pool_stage(nc, P, xin, g16, xout, pool_w, pool_scT, edge_d, hmask_d, vec_d, layer, with_ctx=True):
    with ExitStack() as es:
        C = Consts(nc, P, es)
        NB = 512
        xt = [sb(nc, es, "p_xt%d" % i, [128, KC, NB], F32) for i in range(3)]
        sq = sb(nc, es, "p_sq", [128, KC, NB], BF16)
        hf = [sb(nc, es, "p_hf%d" % i, [128, KC, NB], F32) for i in range(2)]
        sA = sb(nc, es, "p_sA", [128, 2, NB], F32)
        sB = sb(nc, es, "p_sB", [128, 2, NB], F32)
        sC = sb(nc, es, "p_sC", [128, 2, NB], F32)
        sD = sb(nc, es, "p_sD", [128, 2, NB], F32)
        b_sC, b_sD = Buf(), Buf()
        tmp = sb(nc, es, "p_tmp", [128, 2, 8], F32)
        df = sb(nc, es, "p_df", [128, KC, NB], BF16)
        rstd = sb(nc, es, "p_rstd", [128, NB], F32)
        xr = [sb(nc, es, "p_xr%d" % i, [128, NB], F32) for i in range(2)]
        vec = sb(nc, es, "p_vec", [128, 3, 2, KC], F32)
        psc = sb(nc, es, "p_psc", [128, KC], F32)
        psg = sb(nc, es, "p_psg", [128, 2, KC], F32)
        edge = sb(nc, es, "p_edge", [128, 4, KC, 8], F32)
        hmask = sb(nc, es, "p_hmask", [128, 2], F32)
        pw = sb(nc, es, "p_pw", [128, 4, 2, 256], BF16)
        ssp = pp(nc, es, "p_ss", [128, NB], F32)
        yps = [pp(nc, es, "p_y%d" % i, [128, NB], F32) for i in range(2)]
        b_xt = [Buf(), Buf(), Buf()]
        b_hf = [Buf(), Buf()]
        b_sq, b_sA, b_sB, b_tmp, b_df, b_rstd, b_vec, b_psc, b_psg, b_edge, b_hm, b_pw, b_ss = (
            Buf() for _ in range(13))
        b_xr = [Buf(), Buf()]
        b_y = [Buf(), Buf()]

        P.op("sync", lambda e: e.dma_start(out=vec[:], in_=vec_d[:, layer, 3:6, :, :]), writes=[b_vec], dma=True)
        P.op("sync", lambda e: e.dma_start(out=psc[:], in_=pool_scT), writes=[b_psc], dma=True)
        P.op("sync", lambda e: e.dma_start(out=edge[:], in_=edge_d), writes=[b_edge], dma=True)
        P.op("sync", lambda e: e.dma_start(out=hmask[:], in_=hmask_d), writes=[b_hm], dma=True)
        P.op("gpsimd", lambda e: e.dma_start(out=pw[:], in_=pool_w.rearrange("g (c p) e -> p g c e", p=128)),
             writes=[b_pw], dma=True)
        for s in range(2):
            P.op("vector", (lambda e, s=s: e.tensor_tensor(out=psg[:, s, :], in0=vec[:, 2, s, :], in1=psc[:],
                                                         op=ALU.mult)),
                 reads=[b_vec, b_psc], writes=[b_psg])

        xiv = xin.rearrange("(k p) t -> p k t", p=128)
        g16v = g16.rearrange("(r k p) t -> r p k t", r=2, p=128)
        xov = xout.rearrange("(k p) t -> p k t", p=128)
        tiles = pool_tiles(with_ctx)

        def norm_part(ti):
            c0, n, s, ledge, redge, oc0 = tiles[ti]
            slot = ti % 3
            x = xt[slot]
            hfc, b_hfc = hf[ti % 2], b_hf[ti % 2]
            nb = n + 16
            lo = 8 if ledge else 0
            hi = 8 + n if redge else nb
            P.op("sync", (lambda e: e.dma_start(out=x[:, :, lo:hi], in_=xiv[:, :, c0 - 8 + lo:c0 - 8 + hi])),
                 writes=[b_xt[slot]], dma=True)
            if s == 1:
                P.op("gpsimd", (lambda e: e.memset(x[:, :, 0:8], 0.0)), writes=[b_xt[slot]])
                P.op("gpsimd", (lambda e: e.memset(x[:, :, 8 + n:16 + n], 0.0)), writes=[b_xt[slot]])
            else:
                if ledge:
                    P.op("sync", (lambda e: e.dma_start(out=x[:, :, 0:8], in_=g16v[0, :, :, 8:16])),
                         writes=[b_xt[slot]], dma=True)
                if redge:
                    P.op("sync", (lambda e: e.dma_start(out=x[:, :, 8 + n:16 + n], in_=g16v[1, :, :, 0:8])),
                         writes=[b_xt[slot]], dma=True)
            emit_norm_mod(nc, P, C, x, b_xt[slot], nb, sq, b_sq, ssp, b_ss, rstd, b_rstd, xr, b_xr,
                          hfc, b_hfc, vec[:, 0, s, :], vec[:, 1, s, :], b_vec, mod_eng="scalar")
            if s == 1:
                P.op("gpsimd", lambda e: e.memset(hfc[:, :, 0:8], 0.0), writes=[b_hfc])
                P.op("gpsimd", (lambda e: e.memset(hfc[:, :, 8 + n:16 + n], 0.0)), writes=[b_hfc])
            else:
                if ledge:
                    P.op("gpsimd", lambda e: e.tensor_scalar(out=hfc[:, :, 0:8], in0=hfc[:, :, 0:8],
                                                             scalar1=hmask[:, 0:1], scalar2=0.0,
                                                             op0=ALU.mult, op1=ALU.add),
                         reads=[b_hm], writes=[b_hfc])
                if redge:
                    P.op("gpsimd", (lambda e: e.tensor_scalar(out=hfc[:, :, 8 + n:16 + n],
                                                              in0=hfc[:, :, 8 + n:16 + n],
                                                              scalar1=hmask[:, 1:2], scalar2=0.0,
                                                              op0=ALU.mult, op1=ALU.add)),
                         reads=[b_hm], writes=[b_hfc])

        def core_part(ti):
            c0, n, s, ledge, redge, oc0 = tiles[ti]
            slot = ti % 3
            x = xt[slot]
            hfc, b_hfc = hf[ti % 2], b_hf[ti % 2]
            nb = n + 16
            for gi, w in enumerate(POOL_W):
                hv = hfc[:, 2 * gi:2 * gi + 2, :]
                cur, bcur, ln = hv, b_hfc, nb
                step = 1
                bufs = [(sA, b_sA), (sB, b_sB)] if gi >= 2 else [(sC, b_sC), (sD, b_sD)]
                aeng = "gpsimd" if gi >= 2 else "vector"
                bi = 0
                while step < w:
                    dst, bdst = bufs[bi]
                    bi ^= 1
                    nl = ln - step
                    P.op(aeng, (lambda e, dst=dst, cur=cur, nl=nl, step=step: e.tensor_tensor(
                        out=dst[:, :, 0:nl], in0=cur[:, :, 0:nl], in1=cur[:, :, step:step + nl], op=ALU.add)),
                        reads=[bcur], writes=[bdst])
                    cur, bcur, ln = dst, bdst, nl
                    step *= 2
                o = 8 - w // 2
                P.op("vector", (lambda e, cur=cur, o=o, w=w, gi=gi: e.scalar_tensor_tensor(
                    out=df[:, 2 * gi:2 * gi + 2, :n], in0=cur[:, :, o:o + n], scalar=1.0 / w,
                    in1=hfc[:, 2 * gi:2 * gi + 2, 8:8 + n], op0=ALU.mult, op1=ALU.subtract)),
                    reads=[bcur, b_hfc], writes=[b_df])
                fixes = []
                if ledge:
                    fixes.append((0, 0 if s == 0 else 2))
                if redge:
                    fixes.append((n - 8, 1 if s == 0 else 3))
                for (q0, tb) in fixes:
                    P.op("vector", (lambda e, cur=cur, o=o, q0=q0, tb=tb, gi=gi: e.tensor_tensor(
                        out=tmp[:], in0=cur[:, :, o + q0:o + q0 + 8], in1=edge[:, tb, 2 * gi:2 * gi + 2, :],
                        op=ALU.mult)), reads=[bcur, b_edge], writes=[b_tmp])
                    P.op("vector", (lambda e, q0=q0, gi=gi: e.tensor_tensor(
                        out=df[:, 2 * gi:2 * gi + 2, q0:q0 + 8], in0=tmp[:],
                        in1=hfc[:, 2 * gi:2 * gi + 2, 8 + q0:16 + q0], op=ALU.subtract)),
                        reads=[b_tmp, b_hfc], writes=[b_df])
            for gi in range(4):
                for ec in range(2):
                    d = 2 * gi + ec
                    j = d % 2
                    for cc in range(2):
                        P.op("tensor", (lambda e, j=j, gi=gi, ec=ec, cc=cc: e.matmul(
                            yps[j][:, :n], lhsT=pw[:, gi, cc, ec * 128:(ec + 1) * 128], rhs=df[:, 2 * gi + cc, :n],
                            start=(cc == 0), stop=(cc == 1))),
                            reads=[b_pw, b_df], writes=[b_y[j]])
                    P.op("vector", (lambda e, j=j, d=d: e.scalar_tensor_tensor(
                        out=x[:, d, 8:8 + n], in0=yps[j][:, :n], scalar=psg[:, s, d:d + 1],
                        in1=x[:, d, 8:8 + n], op0=ALU.mult, op1=ALU.add)),
                        reads=[b_y[j], b_psg, b_xt[slot]], writes=[b_xt[slot]])
            P.op("sync", (lambda e: e.dma_start(out=xov[:, :, oc0:oc0 + n], in_=x[:, :, 8:8 + n])),
                 reads=[b_xt[slot]], dma=True)

        norm_part(0)
        for ti in range(len(tiles)):
            if ti + 1 < len(tiles):
                norm_part(ti + 1)
            core_part(ti)
        P.flush()


def fm(v):
    v = np.asarray(v, np.float32)
    lead = v.shape[:-1]
    a = v.reshape(lead + (KC, 128))
    return np.ascontiguousarray(np.moveaxis(a, -1, 0))


def pool_edge_tables(half):
    n = 2 * NLAT
    e = np.zeros((128, 4, KC, 8), np.float32)
    for k in range(KC):
        w = POOL_W[k // 2]
        for i in range(8):
            t = i
            cl = min(t, w // 2) + w // 2
            t = n - 8 + i
            cr = min(t + w // 2 - 1, n - 1) - (t - w // 2) + 1
            e[:, 0, k, i] = 1.0 / cl if half == 0 else 1.0 / w
            e[:, 1, k, i] = 1.0 / cr if half == 1 else 1.0 / w
            t = i
            e[:, 2, k, i] = 1.0 / (min(t, w // 2) + w // 2)
            t = NCTX - 8 + i
            e[:, 3, k, i] = 1.0 / (min(t + w // 2 - 1, NCTX - 1) - (t - w // 2) + 1)
    return e


def pool_halo_layout(xs, c):
    half = c % 2
    me, partner = xs[c], xs[c ^ 1]
    z = np.zeros((D, 8), np.float32)
    left = z if half == 0 else partner[:, NLAT - 8:NLAT]
    right = partner[:, 0:8] if half == 0 else z
    return np.ascontiguousarray(np.concatenate([left, me[:, :NLAT], right, z, me[:, NLAT:], z], axis=1))


def qkv_stage(nc, P, xin, w_qkv, bqkT, vec_d, layer, cosT, snT, q_scale, nblk, qT, kTl, kTc, vl, vc, stats_d, tiles):
    rope = cosT is not None
    with ExitStack() as es:
        C = Consts(nc, P, es)
        NT = 512
        w = sb(nc, es, "q_w", [128, KC, 3 * D], BF16)
        xt = [sb(nc, es, "q_xt%d" % i, [128, KC, NT], F32) for i in range(2)]
        hb = sb(nc, es, "q_hb", [128, KC, NT], BF16)
        rstd = sb(nc, es, "q_rstd", [128, NT], F32)
        xr = [sb(nc, es, "q_xr%d" % i, [128, NT], F32) for i in range(2)]
        t1 = [sb(nc, es, "q_t1%d" % i, [128, NT], F32) for i in range(2)]
        t2 = [sb(nc, es, "q_t2%d" % i, [128, NT], F32) for i in range(2)]
        ob = [sb(nc, es, "q_ob%d" % i, [128, NT], BF16) for i in range(3)]
        sqb = sb(nc, es, "q_sqb", [128, NT], BF16)
        vb = [sb(nc, es, "q_vb%d" % i, [128, NT], BF16) for i in range(2)]
        vec = sb(nc, es, "q_vec", [128, 3, 2, KC], F32)
        bq = sb(nc, es, "q_bq", [128, 16], F32)
        stats = sb(nc, es, "q_stats", [128, 16], F32)
        mtmp = sb(nc, es, "q_mtmp", [128, 1], F32)
        blk = sb(nc, es, "q_blk", [128, 128], BF16)
        psA = [pp(nc, es, "q_pa%d" % i, [128, NT], F32) for i in range(2)]
        psB = [pp(nc, es, "q_pb%d" % i, [128, NT], F32) for i in range(2)] if rope else None
        psv = [pp(nc, es, "q_pv%d" % i, [128, NT], F32) for i in range(2)]
        ssp = pp(nc, es, "q_ss", [128, NT], F32)
        nsp = pp(nc, es, "q_ns", [128, NT], F32)
        b_w = [Buf() for _ in range(6)]
        b_xt = [Buf(), Buf()]
        b_hb, b_rstd, b_vec, b_bq, b_stats, b_mtmp, b_blk, b_ss, b_ns, b_sqb, b_wp, b_cs = (Buf() for _ in range(12))
        b_xr = [Buf(), Buf()]
        b_t1 = [Buf(), Buf()]
        b_t2 = [Buf(), Buf()]
        b_ob = [Buf() for _ in range(3)]
        b_vb = [Buf(), Buf()]
        b_pa = [Buf(), Buf()]
        b_pb = [Buf(), Buf()]
        b_pv = [Buf(), Buf()]

        P.op("sync", lambda e: e.dma_start(out=vec[:], in_=vec_d[:, layer, 3:6, :, :]), writes=[b_vec], dma=True)
        P.op("sync", lambda e: e.dma_start(out=bq[:], in_=bqkT), writes=[b_bq], dma=True)
        P.op("vector", lambda e: e.memset(stats[:], 0.0), writes=[b_stats])
        P.op("vector", lambda e: e.memset(blk[:], 0.0), writes=[b_blk])
        bs = 128 // nblk
        for i in range(nblk):
            P.op("vector", (lambda e, i=i: e.memset(blk[i * bs:(i + 1) * bs, i * bs:(i + 1) * bs], 1.0)),
                 writes=[b_blk])
        wv_ = w_qkv.rearrange("(k p) n -> p k n", p=128)
        for i in range(6):
            P.op("gpsimd", (lambda e, i=i: e.dma_start(out=w[:, :, i * 512:(i + 1) * 512],
                                                        in_=wv_[:, :, i * 512:(i + 1) * 512])),
                 writes=[b_w[i]], dma=True)
        if rope:
            wp = sb(nc, es, "q_wp", [128, KC, 2 * D], BF16)
            cs = sb(nc, es, "q_cs", [128, 2, NLAT], F32)
            P.op("sync", lambda e: e.dma_start(out=cs[:, 0, :], in_=cosT), writes=[b_cs], dma=True)
            P.op("sync", lambda e: e.dma_start(out=cs[:, 1, :], in_=snT), writes=[b_cs], dma=True)
            for k in range(KC):
                src = w[:, k, 0:2 * D].rearrange("p (b h s) -> p b h s", h=2, s=16)
                dst = wp[:, k, :].rearrange("p (b h s) -> p b h s", h=2, s=16)
                for hh in range(2):
                    eng = "gpsimd" if (2 * k + hh) % 2 == 0 else "scalar"
                    if eng == "gpsimd":
                        P.op(eng, (lambda e, dst=dst, src=src, hh=hh: e.tensor_copy(out=dst[:, :, hh, :],
                                                                                in_=src[:, :, 1 - hh, :])),
                             reads=b_w[0:4], writes=[b_wp])
                    else:
                        P.op(eng, (lambda e, dst=dst, src=src, hh=hh: e.activation(out=dst[:, :, hh, :],
                                                                                in_=src[:, :, 1 - hh, :],
                                                                                func=AF.Copy)),
                             reads=b_w[0:4], writes=[b_wp])

        xiv = xin.rearrange("(k p) t -> p k t", p=128)
        ci = 0
        vi = 0
        deferred = []
        for ti, (c0, n, s) in enumerate(tiles):
            slot = ti % 2
            x = xt[slot]
            P.op("sync", (lambda e, x=x, c0=c0, n=n: e.dma_start(out=x[:, :, :n], in_=xiv[:, :, c0:c0 + n])),
                 writes=[b_xt[slot]], dma=True)
            emit_norm_mod(nc, P, C, x, b_xt[slot], n, hb, b_hb, ssp, b_ss, rstd, b_rstd, xr, b_xr,
                          hb, b_hb, vec[:, 0, s, :], vec[:, 1, s, :], b_vec)
            dorope = rope and s == 0
            for c in range(16):
                j = ci % 2
                oj = ci % 3
                ci += 1
                for k in range(KC):
                    P.op("tensor", (lambda e, j=j, c=c, k=k, n=n: e.matmul(
                        psA[j][:, :n], lhsT=w[:, k, c * 128:(c + 1) * 128], rhs=hb[:, k, :n],
                        start=(k == 0), stop=(k == KC - 1))),
                        reads=[b_w[c // 4], b_hb], writes=[b_pa[j]])
                if dorope:
                    for k in range(KC):
                        P.op("tensor", (lambda e, j=j, c=c, k=k, n=n: e.matmul(
                            psB[j][:, :n], lhsT=wp[:, k, c * 128:(c + 1) * 128], rhs=hb[:, k, :n],
                            start=(k == 0), stop=(k == KC - 1))),
                            reads=[b_wp, b_hb], writes=[b_pb[j]])
                    P.op("vector", (lambda e, j=j, c0=c0, n=n: e.tensor_tensor(
                        out=t1[j][:, :n], in0=psA[j][:, :n], in1=cs[:, 0, c0:c0 + n], op=ALU.mult)),
                        reads=[b_pa[j], b_cs], writes=[b_t1[j]])
                    P.op("vector", (lambda e, j=j, c0=c0, n=n: e.tensor_tensor(
                        out=t2[j][:, :n], in0=psB[j][:, :n], in1=cs[:, 1, c0:c0 + n], op=ALU.mult)),
                        reads=[b_pb[j], b_cs], writes=[b_t2[j]])
                    P.op("gpsimd", (lambda e, j=j, oj=oj, n=n: e.tensor_tensor(
                        out=ob[oj][:, :n], in0=t1[j][:, :n], in1=t2[j][:, :n], op=ALU.add)),
                        reads=[b_t1[j], b_t2[j]], writes=[b_ob[oj]])
                else:
                    sc = q_scale if c < 8 else 1.0
                    P.op("vector", (lambda e, j=j, oj=oj, c=c, n=n, sc=sc: e.tensor_scalar(
                        out=ob[oj][:, :n], in0=psA[j][:, :n], scalar1=bq[:, c:c + 1], scalar2=sc,
                        op0=ALU.add, op1=ALU.mult)),
                        reads=[b_pa[j], b_bq], writes=[b_ob[oj]])
                if c < 8:
                    dst = qT[c * 128:(c + 1) * 128, c0:c0 + n]
                elif s == 0 and isinstance(kTl, list):
                    dst = kTl[ti][(c - 8) * 128:(c - 7) * 128, 0:n]
                elif s == 0:
                    dst = kTl[(c - 8) * 128:(c - 7) * 128, c0:c0 + n]
                else:
                    dst = kTc[(c - 8) * 128:(c - 7) * 128, 0:n]
                P.op("sync", (lambda e, dst=dst, oj=oj, n=n: e.dma_start(out=dst, in_=ob[oj][:, :n])),
                     reads=[b_ob[oj]], dma=True)
                def stats_ops(oj=oj, n=n, c=c):
                    P.op("scalar", _act(sqb[:, :n], ob[oj][:, :n], AF.Square), reads=[b_ob[oj]], writes=[b_sqb])
                    P.op("tensor", (lambda e: e.matmul(nsp[:, :n], lhsT=blk[:], rhs=sqb[:, :n], start=True, stop=True)),
                         reads=[b_blk, b_sqb], writes=[b_ns])
                    P.op("vector", (lambda e: e.tensor_reduce(out=mtmp[:], in_=nsp[:, :n],
                                                              axis=mybir.AxisListType.X, op=ALU.max)),
                         reads=[b_ns], writes=[b_mtmp])
                    P.op("vector", (lambda e: e.tensor_tensor(out=stats[:, c:c + 1], in0=stats[:, c:c + 1],
                                                             in1=mtmp[:], op=ALU.max)),
                         reads=[b_mtmp, b_stats], writes=[b_stats])
                if deferred:
                    deferred.pop(0)()
                deferred.append(stats_ops)
            while deferred:
                deferred.pop(0)()
            for sub in range(n // 128):
                for vh in range(2):
                    j = vi % 2
                    vi += 1
                    for k in range(KC):
                        P.op("tensor", (lambda e, j=j, k=k, sub=sub, vh=vh: e.matmul(
                            psv[j][:, :], lhsT=hb[:, k, sub * 128:(sub + 1) * 128],
                            rhs=w[:, k, 2 * D + vh * 512:2 * D + (vh + 1) * 512],
                            start=(k == 0), stop=(k == KC - 1))),
                            reads=[b_w[4 + vh], b_hb], writes=[b_pv[j]])
                    P.op("scalar", _act(vb[j][:], psv[j][:], AF.Copy), reads=[b_pv[j]], writes=[b_vb[j]])
                    if s == 0 and isinstance(vl, list):
                        vdst, r0 = vl[ti], sub * 128
                    else:
                        vdst = vl if s == 0 else vc
                        r0 = (c0 if s == 0 else 0) + sub * 128
                    P.op("sync", (lambda e, j=j, r0=r0, vh=vh, vdst=vdst: e.dma_start(
                        out=vdst[r0:r0 + 128, vh * 512:(vh + 1) * 512], in_=vb[j][:])),
                        reads=[b_vb[j]], dma=True)
        P.op("sync", lambda e: e.dma_start(out=stats_d, in_=stats[:]), reads=[b_stats], dma=True)
        P.flush()


DIFF_HEADS = 8
DIFF_SCALE = 64 ** -0.5


def diff_attn_stage(nc, P, qT, k_pieces, v_pieces, st_q, st_k0, st_k1, lamR, sublnT, lam_init, oT, nkey_lat):
    NKC_LAT = nkey_lat // 128
    NKC = NKC_LAT + NCTX // 128
    NKEY = NKC * 128
    with ExitStack() as es:
        C = Consts(nc, P, es)
        NT = 512
        kh = [sb(nc, es, "a_k%d" % i, [128, NKEY], BF16) for i in range(2)]
        vh = [sb(nc, es, "a_v%d" % i, [128, NKC, 128], BF16) for i in range(2)]
        qz = [[sb(nc, es, "a_q%d%d" % (i, j), [128, NTOK], BF16) for j in range(2)] for i in range(2)]
        pT = [sb(nc, es, "a_p%d" % i, [128, 2, NT], BF16) for i in range(3)]
        rz = [sb(nc, es, "a_rz%d" % i, [128, NT], F32) for i in range(2)]
        ta = sb(nc, es, "a_ta", [128, NT], F32)
        tb = sb(nc, es, "a_tb", [128, NT], F32)
        av = [sb(nc, es, "a_a%d" % i, [128, NT], F32) for i in range(2)]
        sqb = sb(nc, es, "a_sq", [128, NT], BF16)
        rs = sb(nc, es, "a_rs", [128, NT], F32)
        ob = [sb(nc, es, "a_ob%d" % i, [128, NT], BF16) for i in range(2)]
        sq_t = sb(nc, es, "a_stq", [128, 16], F32)
        sk0_t = sb(nc, es, "a_stk0", [128, 16], F32)
        sk1_t = sb(nc, es, "a_stk1", [128, 16], F32)
        nB = sb(nc, es, "a_nB", [128, 8], F32)
        lam_t = sb(nc, es, "a_lam", [128, 4, 64], F32)
        lp = sb(nc, es, "a_lp", [128, 2, 64], F32)
        ls = sb(nc, es, "a_ls", [128, 2], F32)
        nl = sb(nc, es, "a_nl", [128, 1], F32)
        sgs = sb(nc, es, "a_sgs", [128, 1], F32)
        sps = [pp(nc, es, "a_s%d" % i, [128, 2, NT], F32) for i in range(2)]
        pvp = [pp(nc, es, "a_pv%d" % i, [128, NT], F32) for i in range(2)]
        zp = [pp(nc, es, "a_z%d" % i, [128, NT], F32) for i in range(2)]
        ssp = zp[1]
        b_k = [Buf(), Buf()]
        b_v = [Buf(), Buf()]
        b_q = [Buf(), Buf()]
        b_p = [Buf() for _ in range(3)]
        b_rz = [Buf(), Buf()]
        b_ta, b_tb, b_sqb, b_rs, b_st, b_nB, b_lam, b_lp, b_ls, b_nl, b_sgs, b_ss = (Buf() for _ in range(12))
        b_a = [Buf(), Buf()]
        b_ob = [Buf(), Buf()]
        b_s = [Buf(), Buf()]
        b_pv = [Buf(), Buf()]
        b_z = [Buf(), Buf()]
        b_ss = b_z[1]

        for i in range(2):
            for j in range(2):
                P.op("gpsimd", (lambda e, i=i, j=j: e.memset(qz[i][j][:], 0.0)), writes=[b_q[i]])
        P.op("sync", lambda e: e.dma_start(out=sq_t[:], in_=st_q), writes=[b_st], dma=True)
        P.op("sync", lambda e: e.dma_start(out=sk0_t[:], in_=st_k0), writes=[b_st], dma=True)
        P.op("sync", lambda e: e.dma_start(out=sk1_t[:], in_=st_k1), writes=[b_st], dma=True)
        P.op("sync", lambda e: e.dma_start(out=lam_t[:], in_=lamR), writes=[b_lam], dma=True)
        P.op("sync", lambda e: e.dma_start(out=sgs[:], in_=sublnT), writes=[b_sgs], dma=True)
        P.op("vector", lambda e: e.tensor_tensor(out=sk0_t[:, 8:16], in0=sk0_t[:, 8:16], in1=sk1_t[:, 8:16],
                                                 op=ALU.max), reads=[b_st], writes=[b_st])
        P.op("vector", lambda e: e.tensor_tensor(out=nB[:], in0=sq_t[:, 0:8], in1=sk0_t[:, 8:16], op=ALU.mult),
             reads=[b_st], writes=[b_nB])
        P.op("scalar", _act(nB[:], nB[:], AF.Sqrt), reads=[b_nB], writes=[b_nB])
        P.op("vector", lambda e: e.tensor_scalar(out=nB[:], in0=nB[:], scalar1=-DIFF_SCALE, scalar2=None,
                                                 op0=ALU.mult), reads=[b_nB], writes=[b_nB])
        for i in range(2):
            P.op("vector", (lambda e, i=i: e.tensor_tensor(out=lp[:, i, :], in0=lam_t[:, 2 * i, :],
                                                         in1=lam_t[:, 2 * i + 1, :], op=ALU.mult)),
                 reads=[b_lam], writes=[b_lp])
            P.op("vector", (lambda e, i=i: e.tensor_reduce(out=ls[:, i:i + 1], in_=lp[:, i, :],
                                                         axis=mybir.AxisListType.X, op=ALU.add)),
                 reads=[b_lp], writes=[b_ls])
        P.op("scalar", _act(ls[:], ls[:], AF.Exp), reads=[b_ls], writes=[b_ls])
        P.op("vector", lambda e: e.tensor_tensor(out=nl[:], in0=ls[:, 1:2], in1=ls[:, 0:1], op=ALU.subtract),
             reads=[b_ls], writes=[b_nl])
        P.op("vector", lambda e: e.tensor_scalar(out=nl[:], in0=nl[:], scalar1=-float(lam_init), scalar2=None,
                                                 op0=ALU.add), reads=[b_nl], writes=[b_nl])
        P.op("vector", lambda e: e.tensor_scalar(out=sgs[:], in0=sgs[:], scalar1=1.0 - float(lam_init),
                                                 scalar2=None, op0=ALU.mult), reads=[b_sgs], writes=[b_sgs])

        tiles = lat_ctx_tiles()
        pending = []
        cnt = {"pi": 0, "si": 0, "ai": 0}

        def do_j(h, hs, c0, n, kcs, j):
            pr = slice(j * 64, (j + 1) * 64)

            def mm1pair(pair, sj):
                for i, kc in enumerate(pair):
                    P.op("tensor", (lambda e, i=i, kc=kc: e.matmul(
                        sps[sj][:, i, :n], lhsT=kh[hs][:, kc * 128:(kc + 1) * 128], rhs=qz[hs][j][:, c0:c0 + n],
                        start=True, stop=True)),
                        reads=[b_k[hs], b_q[hs]], writes=[b_s[sj]])

            def unit(pair, sj, pj, first, last):
                np_ = len(pair)
                P.op("scalar", _act(pT[pj][:, :np_, :n], sps[sj][:, :np_, :n], AF.Exp, bias=nB[:, h:h + 1],
                                    scale=DIFF_SCALE),
                     reads=[b_s[sj], b_nB], writes=[b_p[pj]])
                for i, kc in enumerate(pair):
                    st_ = first and i == 0
                    en_ = last and i == np_ - 1
                    P.op("tensor", (lambda e, i=i, kc=kc, st_=st_, en_=en_: e.matmul(
                        pvp[j][:, :n], lhsT=vh[hs][:, kc, :], rhs=pT[pj][:, i, :n], start=st_, stop=en_)),
                        reads=[b_v[hs], b_p[pj]], writes=[b_pv[j]])
                    P.op("tensor", (lambda e, i=i, st_=st_, en_=en_: e.matmul(
                        zp[j][:, :n], lhsT=C.ones_bf[:], rhs=pT[pj][:, i, :n], start=st_, stop=en_)),
                        reads=[C.b_ones, b_p[pj]], writes=[b_z[j]])

            pairs = [kcs[i:i + 2] for i in range(0, len(kcs), 2)]
            mm1pair(pairs[0], cnt["si"] % 2)
            for idx, pair in enumerate(pairs):
                sj = cnt["si"] % 2
                cnt["si"] += 1
                if idx + 1 < len(pairs):
                    mm1pair(pairs[idx + 1], cnt["si"] % 2)
                pj = cnt["pi"] % 3
                cnt["pi"] += 1
                unit(pair, sj, pj, idx == 0, idx == len(pairs) - 1)
                if pending and j == 0 and (idx == min(12, len(pairs) - 1)):
                    pending.pop(0)()
            P.op("vector", (lambda e: e.reciprocal(out=rz[j][:, :n], in_=zp[j][:, :n])),
                 reads=[b_z[j]], writes=[b_rz[j]])
            if j == 0:
                P.op("vector", lambda e: e.tensor_tensor(out=ta[:, :n], in0=pvp[0][:, :n], in1=rz[0][:, :n],
                                                         op=ALU.mult),
                     reads=[b_pv[0], b_rz[0]], writes=[b_ta])
            else:
                P.op("vector", lambda e: e.scalar_tensor_tensor(
                    out=tb[:, :n], in0=pvp[1][:, :n], scalar=nl[:, 0:1], in1=rz[1][:, :n],
                    op0=ALU.mult, op1=ALU.mult),
                    reads=[b_pv[1], b_rz[1], b_nl], writes=[b_tb])

        def do_tile(h, hs, c0, n, s):
            kcs = list(range(NKC)) if s == 0 else list(range(NKC_LAT, NKC))
            acur = cnt["ai"] % 2
            cnt["ai"] += 1
            for j in range(2):
                do_j(h, hs, c0, n, kcs, j)
            a = av[acur]
            P.op("gpsimd", (lambda e: e.tensor_tensor(out=a[:, :n], in0=ta[:, :n], in1=tb[:, :n], op=ALU.add)),
                 reads=[b_ta, b_tb], writes=[b_a[acur]])
            P.op("scalar", _act(sqb[:, :n], a[:, :n], AF.Square), reads=[b_a[acur]], writes=[b_sqb])

            def fin():
                P.op("tensor", (lambda e: e.matmul(ssp[:, :n], lhsT=C.ones_bf[:], rhs=sqb[:, :n],
                                                   start=True, stop=True)),
                     reads=[C.b_ones, b_sqb], writes=[b_ss])
                P.op("scalar", _act(rs[:, :n], ssp[:, :n], AF.Sqrt, bias=EPS, scale=1.0 / 128.0),
                     reads=[b_ss], writes=[b_rs])
                P.op("vector", (lambda e: e.reciprocal(out=rs[:, :n], in_=rs[:, :n])), reads=[b_rs], writes=[b_rs])
                P.op("vector", (lambda e: e.scalar_tensor_tensor(
                    out=ob[acur][:, :n], in0=a[:, :n], scalar=sgs[:, 0:1], in1=rs[:, :n],
                    op0=ALU.mult, op1=ALU.mult)),
                    reads=[b_a[acur], b_sgs, b_rs], writes=[b_ob[acur]])
                P.op("sync", (lambda e: e.dma_start(out=oT[h * 128:(h + 1) * 128, c0:c0 + n], in_=ob[acur][:, :n])),
                     reads=[b_ob[acur]], dma=True)
            pending.append(fin)

        def load_head(h):
            hs = h % 2
            col = 0
            for (kap, ncols) in k_pieces:
                P.op("sync", (lambda e, kap=kap, col=col, ncols=ncols: e.dma_start(
                    out=kh[hs][:, col:col + ncols], in_=kap[h * 128:(h + 1) * 128, :])),
                    writes=[b_k[hs]], dma=True)
                col += ncols
            for j in range(2):
                P.op("sync", (lambda e, j=j: e.dma_start(out=qz[hs][j][j * 64:(j + 1) * 64, :],
                                                         in_=qT[h * 128 + j * 64:h * 128 + (j + 1) * 64, :])),
                     writes=[b_q[hs]], dma=True)
            ch = 0
            for (vap, nrows) in v_pieces:
                nch = nrows // 128
                step = 16
                for c1 in range(0, nch, step):
                    c2 = min(nch, c1 + step)
                    vv = vap[c1 * 128:c2 * 128, h * 128:(h + 1) * 128].rearrange("(c p) e -> p c e", p=128)
                    P.op("gpsimd", (lambda e, vv=vv, a=ch + c1, b=ch + c2: e.dma_start(out=vh[hs][:, a:b, :], in_=vv)),
                         writes=[b_v[hs]], dma=True)
                ch += nch

        load_head(0)
        for h in range(DIFF_HEADS):
            if h + 1 < DIFF_HEADS:
                load_head(h + 1)
            for (c0, n, s) in tiles:
                do_tile(h, h % 2, c0, n, s)
        while pending:
            pending.pop(0)()
        P.flush()


def proj_stage(nc, P, oT, w_o, boT, xin, xout, vec_d, layer, tiles):
    with ExitStack() as es:
        NT = 512
        w = sb(nc, es, "o_w", [128, KC, D], BF16)
        ot = [sb(nc, es, "o_ot%d" % i, [128, KC, NT], BF16) for i in range(2)]
        xt = [sb(nc, es, "o_xt%d" % i, [128, KC, NT], F32) for i in range(2)]
        vec = sb(nc, es, "o_vec", [128, 2, KC], F32)
        bo = sb(nc, es, "o_bo", [128, KC], F32)
        gb = sb(nc, es, "o_gb", [128, 2, KC], F32)
        yps = [pp(nc, es, "o_y%d" % i, [128, NT], F32) for i in range(2)]
        b_w, b_vec, b_bo, b_gb = Buf(), Buf(), Buf(), Buf()
        b_ot = [Buf(), Buf()]
        b_xt = [Buf(), Buf()]
        b_y = [Buf(), Buf()]
        P.op("gpsimd", lambda e: e.dma_start(out=w[:], in_=w_o.rearrange("(k p) n -> p k n", p=128)),
             writes=[b_w], dma=True)
        P.op("sync", lambda e: e.dma_start(out=vec[:], in_=vec_d[:, layer, 5, :, :]), writes=[b_vec], dma=True)
        P.op("sync", lambda e: e.dma_start(out=bo[:], in_=boT), writes=[b_bo], dma=True)
        for s in range(2):
            P.op("vector", (lambda e, s=s: e.tensor_tensor(out=gb[:, s, :], in0=vec[:, s, :], in1=bo[:], op=ALU.mult)),
                 reads=[b_vec, b_bo], writes=[b_gb])
        xiv = xin.rearrange("(k p) t -> p k t", p=128)
        xov = xout.rearrange("(k p) t -> p k t", p=128)
        ov = oT.rearrange("(k p) t -> p k t", p=128)
        for ti, (c0, n, s) in enumerate(tiles):
            slot = ti % 2
            x = xt[slot]
            o = ot[slot]
            P.op("sync", (lambda e, x=x, c0=c0, n=n: e.dma_start(out=x[:, :, :n], in_=xiv[:, :, c0:c0 + n])),
                 writes=[b_xt[slot]], dma=True)
            P.op("sync", (lambda e, o=o, c0=c0, n=n: e.dma_start(out=o[:, :, :n], in_=ov[:, :, c0:c0 + n])),
                 writes=[b_ot[slot]], dma=True)
            for d in range(KC):
                j = d % 2
                for k in range(KC):
                    P.op("tensor", (lambda e, j=j, k=k, d=d, o=o, n=n: e.matmul(
                        yps[j][:, :n], lhsT=w[:, k, d * 128:(d + 1) * 128], rhs=o[:, k, :n],
                        start=(k == 0), stop=(k == KC - 1))),
                        reads=[b_w, b_ot[slot]], writes=[b_y[j]])
                P.op("vector", (lambda e, x=x, j=j, d=d, n=n, s=s: e.scalar_tensor_tensor(
                    out=x[:, d, :n], in0=yps[j][:, :n], scalar=vec[:, s, d:d + 1], in1=x[:, d, :n],
                    op0=ALU.mult, op1=ALU.add)),
                    reads=[b_y[j], b_vec, b_xt[slot]], writes=[b_xt[slot]])
                P.op("gpsimd", (lambda e, x=x, d=d, n=n, s=s: e.tensor_scalar(
                    out=x[:, d, :n], in0=x[:, d, :n], scalar1=1.0, scalar2=gb[:, s, d:d + 1],
                    op0=ALU.mult, op1=ALU.add)),
                    reads=[b_gb, b_xt[slot]], writes=[b_xt[slot]])
            P.op("sync", (lambda e, x=x, c0=c0, n=n: e.dma_start(out=xov[:, :, c0:c0 + n], in_=x[:, :, :n])),
                 reads=[b_xt[slot]], dma=True)
        P.flush()


def rope_tables(half):
    t = np.arange(half * NLAT, (half + 1) * NLAT)
    rows = (t // 64).astype(np.float32)
    cols = (t % 64).astype(np.float32)
    inv = (np.float32(10000.0) ** (-np.arange(0, 32, 2, dtype=np.float32) / np.float32(32))).astype(np.float32)
    cosT = np.zeros((128, NLAT), np.float32)
    snT = np.zeros((128, NLAT), np.float32)
    for p in range(128):
        d = p % 64
        pos = rows if d < 32 else cols
        dd = d % 32
        f = dd % 16
        ang = (pos * inv[f]).astype(np.float32)
        cosT[p] = np.cos(ang)
        snT[p] = np.sin(ang) * (-1.0 if dd < 16 else 1.0)
    return cosT, snT


NAT_HEADS = 16
NEXT_ROWS = 72
NEXT = NEXT_ROWS * 64 + NCTX
NEG = -30000.0


def nat_attn_stage(nc, P, qT, kTl, kTc, gkh, vl, vc, gvh, st_q, st_k0, st_k1, rpbR, biasI, biasE, bvH, ident_d, oT):
    NCH = NEXT // 128
    NOD = NEXT_ROWS // 2 - 1
    with ExitStack() as es:
        NT = 512
        ones64 = sb(nc, es, "n_ones", [128, 128], BF16)
        ident = sb(nc, es, "n_id", [128, 128], BF16)
        qz = [[sb(nc, es, "n_q%d%d" % (i, j), [128, NTOK], BF16) for j in range(2)] for i in range(2)]
        kc = [sb(nc, es, "n_k%d" % i, [128, NEXT], BF16) for i in range(2)]
        ve = [sb(nc, es, "n_ve%d" % i, [128, NCH, 128], BF16) for i in range(2)]
        vo = [sb(nc, es, "n_vo%d" % i, [128, NOD, 128], BF16) for i in range(2)]
        bI = [sb(nc, es, "n_bi%d" % i, [128, 4, 64], BF16) for i in range(2)]
        bE = [sb(nc, es, "n_be%d" % i, [128, 8, 6, 64], BF16) for i in range(2)]
        pTc = [sb(nc, es, "n_pc%d" % i, [128, 2, NT], BF16) for i in range(2)]
        pTr = [sb(nc, es, "n_pr%d" % i, [128, 6, 64], BF16) for i in range(3)]
        rz = sb(nc, es, "n_rz", [128, NT], F32)
        ob = [sb(nc, es, "n_ob%d" % i, [128, NT], BF16) for i in range(2)]
        sq_t = sb(nc, es, "n_stq", [128, 16], F32)
        sk0_t = sb(nc, es, "n_stk0", [128, 16], F32)
        sk1_t = sb(nc, es, "n_stk1", [128, 16], F32)
        nB = sb(nc, es, "n_nB", [128, 8], F32)
        bmax = sb(nc, es, "n_bmax", [128, 1], F32)
        rpb_t = sb(nc, es, "n_rpb", [128, 16 * 15 * 31], F32)
        bv = sb(nc, es, "n_bv", [128, KC], F32)
        spr = [pp(nc, es, "n_sr%d" % i, [128, 8, 64], F32) for i in range(2)]
        spc = [pp(nc, es, "n_sc%d" % i, [128, NT], F32) for i in range(2)]
        ops_ = [pp(nc, es, "n_o%d" % i, [128, NT], F32) for i in range(2)]
        zps = [pp(nc, es, "n_z%d" % i, [128, NT], F32) for i in range(2)]
        b_one, b_id, b_st, b_nB, b_bmax, b_rpb, b_bv, b_rz = (Buf() for _ in range(8))
        b_q = [Buf(), Buf()]
        b_k = [Buf(), Buf()]
        b_ve = [Buf(), Buf()]
        b_vo = [Buf(), Buf()]
        b_bI = [Buf(), Buf()]
        b_bE = [Buf(), Buf()]
        b_pc = [Buf(), Buf()]
        b_pr = [Buf() for _ in range(3)]
        b_ob = [Buf(), Buf()]
        b_sr = [Buf(), Buf()]
        b_sc = [Buf(), Buf()]
        b_o = [Buf(), Buf()]
        b_z = [Buf(), Buf()]

        P.op("vector", lambda e: e.memset(ones64[:], 1.0), writes=[b_one])
        for i in range(2):
            for j in range(2):
                P.op("gpsimd", (lambda e, i=i, j=j: e.memset(qz[i][j][:], 0.0)), writes=[b_q[i]])
        P.op("gpsimd", lambda e: e.dma_start(out=ident[:], in_=ident_d), writes=[b_id], dma=True)
        P.op("sync", lambda e: e.dma_start(out=sq_t[:], in_=st_q), writes=[b_st], dma=True)
        P.op("sync", lambda e: e.dma_start(out=sk0_t[:], in_=st_k0), writes=[b_st], dma=True)
        P.op("sync", lambda e: e.dma_start(out=sk1_t[:], in_=st_k1), writes=[b_st], dma=True)
        P.op("sync", lambda e: e.dma_start(out=rpb_t[:], in_=rpbR), writes=[b_rpb], dma=True)
        P.op("sync", lambda e: e.dma_start(out=bv[:], in_=bvH), writes=[b_bv], dma=True)
        P.op("vector", lambda e: e.tensor_reduce(out=bmax[:], in_=rpb_t[:], axis=mybir.AxisListType.X, op=ALU.max),
             reads=[b_rpb], writes=[b_bmax])
        P.op("vector", lambda e: e.tensor_tensor(out=sk0_t[:, 8:16], in0=sk0_t[:, 8:16], in1=sk1_t[:, 8:16],
                                                 op=ALU.max), reads=[b_st], writes=[b_st])
        P.op("vector", lambda e: e.tensor_tensor(out=nB[:], in0=sq_t[:, 0:8], in1=sk0_t[:, 8:16], op=ALU.mult),
             reads=[b_st], writes=[b_nB])
        P.op("scalar", _act(nB[:], nB[:], AF.Sqrt), reads=[b_nB], writes=[b_nB])
        P.op("vector", lambda e: e.tensor_scalar(out=bmax[:], in0=bmax[:], scalar1=0.0, scalar2=None, op0=ALU.max),
             reads=[b_bmax], writes=[b_bmax])
        P.op("vector", lambda e: e.tensor_scalar(out=nB[:], in0=nB[:], scalar1=bmax[:, 0:1], scalar2=-1.0,
                                                 op0=ALU.add, op1=ALU.mult), reads=[b_nB, b_bmax], writes=[b_nB])

        cnt = {"row": 0, "pr": 0, "tile": 0, "ctx": 0}

        def load_chunk(c):
            cs_ = c % 2
            rows = slice(c * 128, (c + 1) * 128)
            cols = slice(c * 128, (c + 1) * 128)

            def ld(q, out, in_, bb):
                P.op(q, (lambda e: e.dma_start(out=out, in_=in_)), writes=[bb], dma=True)

            for j in range(2):
                ld("sync", qz[cs_][j][j * 64:(j + 1) * 64, :], qT[c * 128 + j * 64:c * 128 + (j + 1) * 64, :], b_q[cs_])
            ld("sync", kc[cs_][:, 0:256], gkh[c * 128:(c + 1) * 128, 256:512], b_k[cs_])
            ld("sync", kc[cs_][:, 256:256 + NLAT], kTl[rows, :], b_k[cs_])
            ld("sync", kc[cs_][:, 256 + NLAT:512 + NLAT], gkh[D + c * 128:D + (c + 1) * 128, 0:256], b_k[cs_])
            ld("sync", kc[cs_][:, 512 + NLAT:], kTc[rows, :], b_k[cs_])

            def tokv(ap, r0, nch):
                return ap[r0:r0 + nch * 128, cols].rearrange("(c p) e -> p c e", p=128)
            ld("sync", ve[cs_][:, 0:2, :], tokv(gvh, 256, 2), b_ve[cs_])
            ld("sync", ve[cs_][:, 2:18, :], tokv(vl, 0, 16), b_ve[cs_])
            ld("sync", ve[cs_][:, 18:34, :], tokv(vl, 2048, 16), b_ve[cs_])
            ld("sync", ve[cs_][:, 34:36, :], tokv(gvh, 512, 2), b_ve[cs_])
            ld("sync", ve[cs_][:, 36:38, :], tokv(vc, 0, 2), b_ve[cs_])
            ld("sync", vo[cs_][:, 0:1, :], tokv(gvh, 320, 1), b_vo[cs_])
            ld("sync", vo[cs_][0:64, 1, :], gvh[448:512, cols], b_vo[cs_])
            ld("sync", vo[cs_][64:128, 1, :], vl[0:64, cols], b_vo[cs_])
            ld("sync", vo[cs_][:, 2:18, :], tokv(vl, 64, 16), b_vo[cs_])
            ld("sync", vo[cs_][:, 18:33, :], tokv(vl, 64 + 2048, 15), b_vo[cs_])
            ld("sync", vo[cs_][0:64, 33, :], vl[NLAT - 64:NLAT, cols], b_vo[cs_])
            ld("sync", vo[cs_][64:128, 33, :], gvh[512:576, cols], b_vo[cs_])
            ld("sync", vo[cs_][:, 34:35, :], tokv(gvh, 576, 1), b_vo[cs_])

        def load_head(h):
            hs = h % 2
            P.op("gpsimd", (lambda e: e.dma_start(out=bI[hs][:], in_=biasI[h])), writes=[b_bI[hs]], dma=True)
            P.op("gpsimd", (lambda e: e.dma_start(out=bE[hs][:], in_=biasE[h])), writes=[b_bE[hs]], dma=True)

        def row_plan(r_l):
            if r_l < 4:
                return [(128 * i, 0, i, ("E", r_l, i)) for i in range(6)]
            if r_l >= 60:
                return [(128 * (30 + i), 0, 30 + i, ("E", r_l - 56, i)) for i in range(6)]
            plan = []
            for i in range(4):
                er = r_l + 2 * i
                if er % 2 == 0:
                    plan.append((er * 64, 0, er // 2, ("I", i)))
                else:
                    plan.append((er * 64, 1, (er - 1) // 2, ("I", i)))
            return plan

        def do_tile(h, c0, n, s):
            cs_ = (h // 2) % 2
            hs = h % 2
            hh = h % 2
            pr = slice(hh * 64, (hh + 1) * 64)
            cc = h // 2
            tj = cnt["tile"] % 2
            cnt["tile"] += 1
            o_ps, z_ps = ops_[tj], zps[tj]
            bo, bz = b_o[tj], b_z[tj]
            xj = cnt["ctx"] % 2
            cnt["ctx"] += 1
            for ci in range(2):
                k0 = NEXT_ROWS * 64 + ci * 128
                sj = ci
                P.op("tensor", (lambda e, k0=k0, sj=sj: e.matmul(
                    spc[sj][:, :n], lhsT=kc[cs_][:, k0:k0 + 128], rhs=qz[cs_][hh][:, c0:c0 + n], start=True, stop=True)),
                    reads=[b_k[cs_], b_q[cs_]], writes=[b_sc[sj]])
                P.op("scalar", _act(pTc[xj][:, ci, :n], spc[sj][:, :n], AF.Exp, bias=nB[:, cc:cc + 1], scale=1.0),
                     reads=[b_sc[sj], b_nB], writes=[b_pc[xj]])

            def ctx_pv(q0, nq, first_is_start, last):
                for ci in range(2):
                    P.op("tensor", (lambda e, ci=ci: e.matmul(
                        o_ps[:, q0:q0 + nq], lhsT=ve[cs_][:, NCH - 2 + ci, :],
                        rhs=pTc[xj][:, ci, q0:q0 + nq], start=(first_is_start and ci == 0), stop=(last and ci == 1))),
                        reads=[b_ve[cs_], b_pc[xj]], writes=[bo])
                    P.op("tensor", (lambda e, ci=ci: e.matmul(
                        z_ps[:, q0:q0 + nq], lhsT=ones64[:], rhs=pTc[xj][:, ci, q0:q0 + nq],
                        start=(first_is_start and ci == 0), stop=(last and ci == 1))),
                        reads=[b_one, b_pc[xj]], writes=[bz])

            if s == 1:
                ctx_pv(0, n, True, True)
            else:
                rows = list(range(c0 // 64, c0 // 64 + n // 64))

                def scores(r_l):
                    rj = cnt["row"] % 2
                    cnt["row"] += 1
                    plan = row_plan(r_l)
                    q0 = r_l * 64
                    for i, (k0, vk, vc, bt) in enumerate(plan):
                        P.op("tensor", (lambda e, i=i, k0=k0: e.matmul(
                            spr[rj][:, i, :], lhsT=kc[cs_][:, k0:k0 + 128], rhs=qz[cs_][hh][:, q0:q0 + 64],
                            start=True, stop=False)),
                            reads=[b_k[cs_], b_q[cs_]], writes=[b_sr[rj]])
                        if bt[0] == "I":
                            btile, bb = bI[hs][:, bt[1], :], b_bI[hs]
                        else:
                            btile, bb = bE[hs][:, bt[1], bt[2], :], b_bE[hs]
                        P.op("tensor", (lambda e, i=i, btile=btile: e.matmul(
                            spr[rj][:, i, :], lhsT=ident[:], rhs=btile, start=False, stop=True)),
                            reads=[b_id, bb], writes=[b_sr[rj]])
                    return rj, plan

                def finish_row(r_l, rj, plan, rr):
                    np_ = len(plan)
                    pj = cnt["pr"] % 3
                    cnt["pr"] += 1
                    P.op("scalar", _act(pTr[pj][:, :np_, :], spr[rj][:, :np_, :], AF.Exp, bias=nB[:, cc:cc + 1],
                                        scale=1.0),
                         reads=[b_sr[rj], b_nB], writes=[b_pr[pj]])
                    q0 = rr * 64
                    for i, (k0, vk, vc, bt) in enumerate(plan):
                        vt, bvb = (ve[cs_], b_ve[cs_]) if vk == 0 else (vo[cs_], b_vo[cs_])
                        P.op("tensor", (lambda e, i=i, vt=vt, vc=vc: e.matmul(
                            o_ps[:, q0:q0 + 64], lhsT=vt[:, vc, :], rhs=pTr[pj][:, i, :],
                            start=(i == 0), stop=False)),
                            reads=[bvb, b_pr[pj]], writes=[bo])
                        P.op("tensor", (lambda e, i=i: e.matmul(
                            z_ps[:, q0:q0 + 64], lhsT=ones64[:], rhs=pTr[pj][:, i, :], start=(i == 0), stop=False)),
                            reads=[b_one, b_pr[pj]], writes=[bz])
                    ctx_pv(q0, 64, False, True)

                prev = None
                for rr, r_l in enumerate(rows):
                    rj, plan = scores(r_l)
                    if prev is not None:
                        finish_row(*prev)
                    prev = (r_l, rj, plan, rr)
                finish_row(*prev)
            oj = tj
            P.op("vector", (lambda e: e.reciprocal(out=rz[pr, :n], in_=z_ps[pr, :n])), reads=[bz], writes=[b_rz])
            P.op("vector", (lambda e: e.tensor_tensor(out=rz[pr, :n], in0=o_ps[pr, :n], in1=rz[pr, :n], op=ALU.mult)),
                 reads=[bo, b_rz], writes=[b_rz])
            P.op("gpsimd", (lambda e: e.tensor_scalar(out=ob[oj][pr, :n], in0=rz[pr, :n], scalar1=1.0,
                                                      scalar2=bv[pr, cc:cc + 1], op0=ALU.mult, op1=ALU.add)),
                 reads=[b_rz, b_bv], writes=[b_ob[oj]])
            P.op("sync", (lambda e: e.dma_start(out=oT[h * 64:(h + 1) * 64, c0:c0 + n], in_=ob[oj][pr, :n])),
                 reads=[b_ob[oj]], dma=True)

        tiles = lat_ctx_tiles()
        load_chunk(0)
        load_head(0)
        for h in range(NAT_HEADS):
            if h % 2 == 0 and h // 2 + 1 < KC:
                load_chunk(h // 2 + 1)
            if h + 1 < NAT_HEADS:
                load_head(h + 1)
            for (c0, n, s) in tiles:
                do_tile(h, c0, n, s)
        P.flush()


def nat_bias_tables(rpb, half):
    c = np.arange(64)
    cs = np.clip(c - 8, 0, 48)
    kcol = np.arange(64)
    inwin = (kcol[:, None] >= cs[None, :]) & (kcol[:, None] < cs[None, :] + 16)
    off = np.clip(kcol[:, None] - c[None, :] + 15, 0, 30)
    T = np.where(inwin[None, None], rpb[:, :, off], np.float32(NEG)).astype(np.float32)
    bI = np.empty((16, 128, 4, 64), np.float32)
    for i in range(4):
        for rho in range(2):
            bI[:, rho * 64:(rho + 1) * 64, i, :] = T[:, 2 * i + rho + 3]
    bE = np.full((16, 128, 8, 6, 64), NEG, np.float32)
    for e in range(8):
        r_l = e if e < 4 else 56 + e
        base = 0 if e < 4 else 60
        r = half * 64 + r_l
        rs = min(max(r - 4, 0), 120)
        for i in range(6):
            for rho in range(2):
                gk = half * 64 - 4 + base + 2 * i + rho
                if rs <= gk <= rs + 7:
                    bE[:, rho * 64:(rho + 1) * 64, e, i, :] = T[:, gk - r + 7]
    return bI, bE


def nat_ext_layout(arrs, c, axis):
    half = c % 2
    me, partner = arrs[c], arrs[c ^ 1]
    def tok(a, lo, hi):
        return a[:, lo:hi] if axis == 1 else a[lo:hi]
    z = np.zeros_like(tok(me, 0, 256))
    top = z if half == 0 else tok(partner, NLAT - 256, NLAT)
    bot = tok(partner, 0, 256) if half == 0 else z
    return np.ascontiguousarray(np.concatenate([top, tok(me, 0, NLAT), bot, tok(me, NLAT, NTOK)], axis=axis))


class Launch:
    def __init__(self):
        self.nc = bass.Bass("TRN2", target_bir_lowering=False)
        self.ins = [dict() for _ in range(NCORES)]
        self.outs = []

    def din(self, name, per_core, dt=F32):
        a0 = per_core[0]
        t = self.nc.dram_tensor(name, list(a0.shape), dt, kind="ExternalInput").ap()
        for c in range(NCORES):
            self.ins[c][name] = np.ascontiguousarray(per_core[c])
        return t

    def dout(self, name, shape, dt=F32):
        self.outs.append(name)
        return self.nc.dram_tensor(name, list(shape), dt, kind="ExternalOutput").ap()

    def dint(self, name, shape, dt=F32):
        return self.nc.dram_tensor(name, list(shape), dt).ap()

    def run(self):
        res = run_bass_kernel_spmd(self.nc, self.ins, core_ids=list(range(NCORES)))
        return res.results


def same(a):
    return [a] * NCORES


LAT_TILES = [(c0, 512, 0) for c0 in range(0, NLAT, 512)]


GROUPS = [[0, 1], [2, 3], [4, 5], [6, 7]]


def all_gather(P, src, dst, reads, writes):
    P.cc(lambda e: e.collective_compute("AllGather", ALU.bypass, replica_groups=GROUPS,
                                        ins=[src.opt()], outs=[dst.opt()]),
         reads=reads, writes=writes)


def exchange_edges(nc, P, L, tag, xsrc):
    e16 = L.dint("e16" + tag, [D, 16])
    g16 = L.dint("g16" + tag, [2 * D, 16])
    b_e, b_g = Buf(), Buf()
    P.op("sync", lambda e: e.dma_start(out=e16[:, 0:8], in_=xsrc[:, 0:8]), writes=[b_e], dma=True)
    P.op("sync", lambda e: e.dma_start(out=e16[:, 8:16], in_=xsrc[:, NLAT - 8:NLAT]), writes=[b_e], dma=True)
    all_gather(P, e16, g16, [b_e], [b_g])
    P.flush()
    return g16


def kernel(x, c, ctx, c_ctx, ada_w, ada_b, norm_g, ffn_w_in, ffn_w_out, pool_w, pool_scale,
           diff_w_qkv, diff_lam, diff_subln_g, diff_w_o, nat_w_qkv, nat_b_qkv, nat_rpb,
           nat_w_o, nat_b_o, final_g):
    f32 = lambda a: np.asarray(a, np.float32)
    x, c, ctx, c_ctx = f32(x), f32(c), f32(ctx), f32(c_ctx)
    ada_w, ada_b, norm_g = f32(ada_w), f32(ada_b), f32(norm_g)
    ffn_w_in, ffn_w_out = f32(ffn_w_in), f32(ffn_w_out)
    halves = [cc % 2 for cc in range(NCORES)]
    bats = [cc // 2 for cc in range(NCORES)]
    all_tiles = lat_ctx_tiles()

    xs = [np.concatenate([x[b, h * NLAT:(h + 1) * NLAT], ctx[b]], 0).T for b, h in zip(bats, halves)]
    cT = [fm(np.stack([c[b], c_ctx], 0)).transpose(0, 2, 1) for b in bats]
    ada_bT = np.ascontiguousarray(ada_b.reshape(DEPTH, 72, 128).transpose(2, 0, 1))
    gT = fm(norm_g)
    hmask = []
    for h in halves:
        m = np.zeros((128, 2), np.float32)
        m[:, 0] = 1.0 if h == 1 else 0.0
        m[:, 1] = 1.0 if h == 0 else 0.0
        hmask.append(m)
    edges = [pool_edge_tables(h) for h in halves]
    cos_sn = [rope_tables(h) for h in halves]
    bqkv = f32(nat_b_qkv[0])
    bq_fm = fm(bqkv[:2 * D].reshape(2, D)).reshape(128, 16)
    tabs = [nat_bias_tables(f32(nat_rpb[0]), h) for h in halves]
    lam_init = 0.8 - 0.6 * math.exp(-0.3 * 1)

    L = Launch()
    nc = L.nc
    i_cT = L.din("cT", cT)
    i_ada_w = L.din("ada_w", same(ada_w))
    i_ada_bT = L.din("ada_bT", same(ada_bT))
    i_gT = L.din("gT", same(gT))
    i_x = L.din("xin", xs)
    i_win = [[L.din("w_in%d%d" % (l, u), same(ffn_w_in[l, u])) for u in range(2)] for l in range(DEPTH)]
    i_wout = [[L.din("w_out%d%d" % (l, u), same(ffn_w_out[l, u])) for u in range(2)] for l in range(DEPTH)]
    i_pw = [L.din("pool_w%d" % j, same(f32(pool_w[j]))) for j in range(2)]
    i_psc = [L.din("pscT%d" % j, same(fm(pool_scale[j]))) for j in range(2)]
    i_edge = L.din("edge", edges)
    i_hmask = L.din("hmask", hmask)
    i_dqkv = L.din("d_wqkv", same(f32(diff_w_qkv[0])))
    i_zero16 = L.din("zero16", same(np.zeros((128, 16), np.float32)))
    i_zero8 = L.din("zero8", same(np.zeros((128, KC), np.float32)))
    i_cos = L.din("cosT", [t[0] for t in cos_sn])
    i_sn = L.din("snT", [t[1] for t in cos_sn])
    i_lam = L.din("lamR", same(np.broadcast_to(f32(diff_lam[0]), (128, 4, 64))))
    i_subln = L.din("subln", same(f32(diff_subln_g[0]).reshape(128, 1)))
    i_dwo = L.din("d_wo", same(f32(diff_w_o[0])))
    i_nqkv = L.din("n_wqkv", same(f32(nat_w_qkv[0])))
    i_nbq = L.din("n_bq", same(bq_fm))
    i_rpbR = L.din("rpbR", same(np.broadcast_to(f32(nat_rpb[0]).reshape(1, -1), (128, 7440))))
    i_bI = L.din("bI", [t[0] for t in tabs])
    i_bE = L.din("bE", [t[1] for t in tabs])
    i_bvH = L.din("bvH", same(fm(bqkv[2 * D:])))
    i_ident = L.din("ident", same(np.eye(128, dtype=np.float32)))
    i_nwo = L.din("n_wo", same(f32(nat_w_o[0])))
    i_nbo = L.din("n_bo", same(fm(nat_b_o[0])))
    i_gf = L.din("gfT", same(fm(final_g)))
    yo = L.dout("y", [D, NLAT])

    vec = L.dint("vec", [128, DEPTH, 9, 2, KC])
    X = [L.dint("xs%d" % i, [D, NTOK]) for i in range(12)]

    with ExitStack() as es:
        P = Prog(nc, es)
        mods_stage(nc, P, i_cT, i_ada_w, i_ada_bT, i_gT, vec)
        ffn_stage(nc, P, i_x, X[0], i_win[0][0], i_wout[0][0], vec, 0, 0, all_tiles)
        g16a = exchange_edges(nc, P, L, "a", X[0])
        pool_stage(nc, P, X[0], g16a, X[1], i_pw[0], i_psc[0], i_edge, i_hmask, vec, 0, True)
        ffn_stage(nc, P, X[1], X[2], i_win[0][1], i_wout[0][1], vec, 0, 2, all_tiles)
        ffn_stage(nc, P, X[2], X[3], i_win[1][0], i_wout[1][0], vec, 1, 0, all_tiles)
        qT1 = L.dint("qT1", [D, NTOK], BF16)
        NLT = NLAT // 512
        kTl1 = [L.dint("kTl1_%d" % t, [D, 512], BF16) for t in range(NLT)]
        kTc1 = L.dint("kTc1", [D, NCTX], BF16)
        vl1 = [L.dint("vl1_%d" % t, [512, D], BF16) for t in range(NLT)]
        vc1 = L.dint("vc1", [NCTX, D], BF16)
        st1 = L.dint("st1", [128, 16])
        qkv_stage(nc, P, X[3], i_dqkv, i_zero16, vec, 1, i_cos, i_sn, 1.0, 1, qT1, kTl1, kTc1, vl1, vc1, st1,
                  all_tiles)
        gk = [L.dint("gk%d" % t, [2 * D, 512], BF16) for t in range(NLT)]
        gv = [L.dint("gv%d" % t, [2 * 512, D], BF16) for t in range(NLT)]
        gst1 = L.dint("gst1", [256, 16])
        for t in range(NLT):
            all_gather(P, kTl1[t], gk[t], [], [Buf()])
            all_gather(P, vl1[t], gv[t], [], [Buf()])
        all_gather(P, st1, gst1, [], [Buf()])
        P.flush()
        k_pieces, v_pieces = [], []
        for t in range(NLT):
            k_pieces += [(gk[t][0:D, :], 512), (gk[t][D:2 * D, :], 512)]
            v_pieces += [(gv[t], 1024)]
        k_pieces.append((kTc1, NCTX))
        v_pieces.append((vc1, NCTX))
        oT1 = L.dint("oT1", [D, NTOK], BF16)
        diff_attn_stage(nc, P, qT1, k_pieces, v_pieces, st1, gst1[0:128, :], gst1[128:256, :],
                        i_lam, i_subln, lam_init, oT1, 2 * NLAT)
        proj_stage(nc, P, oT1, i_dwo, i_zero8, X[3], X[4], vec, 1, all_tiles)
        ffn_stage(nc, P, X[4], X[5], i_win[1][1], i_wout[1][1], vec, 1, 2, all_tiles)
        ffn_stage(nc, P, X[5], X[6], i_win[2][0], i_wout[2][0], vec, 2, 0, all_tiles)
        qT2 = L.dint("qT2", [D, NTOK], BF16)
        kTl2 = L.dint("kTl2", [D, NLAT], BF16)
        kTc2 = L.dint("kTc2", [D, NCTX], BF16)
        vl2 = L.dint("vl2", [NLAT, D], BF16)
        vc2 = L.dint("vc2", [NCTX, D], BF16)
        st2 = L.dint("st2", [128, 16])
        qkv_stage(nc, P, X[6], i_nqkv, i_nbq, vec, 2, None, None, 64 ** -0.5, 1, qT2, kTl2, kTc2, vl2, vc2, st2,
                  all_tiles)
        kh = L.dint("khalo", [D, 512], BF16)
        vh = L.dint("vhalo", [512, D], BF16)
        gkh = L.dint("gkh", [2 * D, 512], BF16)
        gvh = L.dint("gvh", [1024, D], BF16)
        gst2 = L.dint("gst2", [256, 16])
        b_kh, b_vh = Buf(), Buf()
        P.op("sync", lambda e: e.dma_start(out=kh[:, 0:256], in_=kTl2[:, 0:256]), writes=[b_kh], dma=True)
        P.op("sync", lambda e: e.dma_start(out=kh[:, 256:512], in_=kTl2[:, NLAT - 256:NLAT]), writes=[b_kh], dma=True)
        P.op("sync", lambda e: e.dma_start(out=vh[0:256, :], in_=vl2[0:256, :]), writes=[b_vh], dma=True)
        P.op("sync", lambda e: e.dma_start(out=vh[256:512, :], in_=vl2[NLAT - 256:NLAT, :]), writes=[b_vh], dma=True)
        all_gather(P, kh, gkh, [b_kh], [Buf()])
        all_gather(P, vh, gvh, [b_vh], [Buf()])
        all_gather(P, st2, gst2, [], [Buf()])
        P.flush()
        oT2 = L.dint("oT2", [D, NTOK], BF16)
        nat_attn_stage(nc, P, qT2, kTl2, kTc2, gkh, vl2, vc2, gvh, st2, gst2[0:128, :], gst2[128:256, :],
                       i_rpbR, i_bI, i_bE, i_bvH, i_ident, oT2)
        proj_stage(nc, P, oT2, i_nwo, i_nbo, X[6], X[7], vec, 2, LAT_TILES)
        ffn_stage(nc, P, X[7], X[8], i_win[2][1], i_wout[2][1], vec, 2, 2, LAT_TILES)
        ffn_stage(nc, P, X[8], X[9], i_win[3][0], i_wout[3][0], vec, 3, 0, LAT_TILES)
        g16b = exchange_edges(nc, P, L, "b", X[9])
        pool_stage(nc, P, X[9], g16b, X[10], i_pw[1], i_psc[1], i_edge, i_hmask, vec, 3, False)
        ffn_stage(nc, P, X[10], yo, i_win[3][1], i_wout[3][1], vec, 3, 2, LAT_TILES, final_g=i_gf)
    res = L.run()

    out = np.empty((4, 2 * NLAT, D), np.float32)
    for cc in range(NCORES):
        out[bats[cc], halves[cc] * NLAT:(halves[cc] + 1) * NLAT, :] = res[cc]["y"].T
    return out
```

```python
import math
from contextlib import ExitStack

import numpy as np
import concourse.bass as bass
import concourse.mybir as mybir
from concourse.bass_utils import run_bass_kernel_spmd

F32 = mybir.dt.float32
BF16 = mybir.dt.bfloat16
F32R = mybir.dt.float32r
AF = mybir.ActivationFunctionType
ALU = mybir.AluOpType

D = 1024
KC = 8
DFF = 2816
FC = 22
NLAT = 4096
NCTX = 256
NTOK = NLAT + NCTX
DEPTH = 4
EPS = 1e-6
NCORES = 8


class Buf:
    __slots__ = ("name", "w", "r")

    def __init__(self, name=""):
        self.name = name
        self.w = None
        self.r = []


class Op:
    __slots__ = ("eng", "fn", "deps", "dma", "sig", "tok", "cc")

    def __init__(self, eng, fn, deps, dma):
        self.eng, self.fn, self.deps, self.dma = eng, fn, deps, dma
        self.sig = False
        self.tok = None
        self.cc = False


STRICT_ALL = False
COMPUTE = ("vector", "scalar", "gpsimd", "tensor")
QUEUES = ("sync", "scalar", "vector", "gpsimd", "tensor")


class Prog:
    def __init__(self, nc, es, n_dma_sems=40):
        self.nc = nc
        self.ops = []
        self.esem = {e: es.enter_context(nc.semaphore("e_" + e)) for e in COMPUTE}
        self.dsem = [es.enter_context(nc.semaphore("d%d" % i)) for i in range(n_dma_sems)]
        self.ccsem = [es.enter_context(nc.semaphore("cc%d" % i)) for i in range(24)]
        self.ccnext = 0
        self.ecnt = {e: 0 for e in COMPUTE}
        self.dcnt = [0] * n_dma_sems
        self.dnext = 0
        self.waited = {q: {} for q in QUEUES}
        self.done_upto = 0

    def op(self, eng, fn, reads=(), writes=(), dma=False, strict=False):
        raw = set()
        deps = set()
        for b in reads:
            if b.w is not None:
                raw.add(b.w)
        for b in writes:
            if b.w is not None:
                deps.add(b.w)
            deps.update(b.r)
        oid = len(self.ops)
        real = []
        for d in raw | deps:
            if d < self.done_upto:
                continue
            o = self.ops[d]
            if (not dma) and (not o.dma) and o.eng == eng:
                if eng == "tensor" or d not in raw:
                    continue
            o.sig = True
            real.append(d)
        self.ops.append(Op(eng, fn, real, dma))
        if dma:
            self.ops[oid].sig = True
        for b in reads:
            b.r.append(oid)
        for b in writes:
            b.w = oid
            b.r = []
        return oid

    def cc(self, fn, reads=(), writes=()):
        oid = self.op("gpsimd", fn, reads=reads, writes=writes, dma=True)
        self.ops[oid].cc = True
        return oid

    def flush(self, final_wait_eng="sync"):
        nc = self.nc
        start = self.done_upto
        ops = self.ops
        per_eng = {q: [] for q in QUEUES}
        plans = []
        for i in range(start, len(ops)):
            o = ops[i]
            waits = []
            w = self.waited[o.eng]
            for d in o.deps:
                sem, val = ops[d].tok
                if w.get(id(sem), (None, 0))[1] >= val:
                    continue
                w[id(sem)] = (sem, val)
                waits.append((sem, val))
            inc = None
            if o.cc:
                sem = self.ccsem[self.ccnext]
                self.ccnext += 1
                o.tok = (sem, 1)
                inc = (sem, None)
            elif o.dma:
                k = self.dnext
                self.dnext = (self.dnext + 1) % len(self.dsem)
                sem = self.dsem[k]
                if self.dcnt[k] > 0 and w.get(id(sem), (None, 0))[1] < self.dcnt[k]:
                    waits.append((sem, self.dcnt[k]))
                    w[id(sem)] = (sem, self.dcnt[k])
                self.dcnt[k] += 16
                o.tok = (sem, self.dcnt[k])
                inc = (sem, 16)
            elif o.sig:
                sem = self.esem[o.eng]
                self.ecnt[o.eng] += 1
                o.tok = (sem, self.ecnt[o.eng])
                inc = (sem, 1)
            per_eng[o.eng].append((o.fn, waits, inc))
        self.done_upto = len(ops)
        finals = [(self.dsem[k], self.dcnt[k]) for k in range(len(self.dsem)) if self.dcnt[k] > 0]
        finals += [(self.ccsem[k], 1) for k in range(self.ccnext)]

        def mk(q, lst, fin):
            def body(e):
                for fn, waits, inc in lst:
                    for sem, val in waits:
                        e.wait_ge(sem, val)
                    ins = fn(e)
                    if inc is not None:
                        if inc[1] is None:
                            ins.then_inc(inc[0])
                        else:
                            ins.then_inc(inc[0], inc[1])
                for sem, val in fin:
                    e.wait_ge(sem, val)
            return body

        with nc.Block() as block:
            for q in QUEUES:
                fin = finals if q == final_wait_eng else []
                if per_eng[q] or fin:
                    getattr(block, q)(mk(q, per_eng[q], fin))
        for q in QUEUES:
            self.waited[q] = dict(self.waited[q])


_uid = [0]


def sb(nc, es, name, shape, dt):
    _uid[0] += 1
    return es.enter_context(nc.sbuf_tensor("%s_%d" % (name, _uid[0]), shape, dt))


def pp(nc, es, name, shape, dt):
    _uid[0] += 1
    return es.enter_context(nc.psum_tensor("%s_%d" % (name, _uid[0]), shape, dt))


def _act(out, in_, func, bias=None, scale=None):
    kw = {}
    if bias is not None:
        kw["bias"] = bias
    if scale is not None:
        kw["scale"] = scale
    return lambda e: e.activation(out=out, in_=in_, func=func, **kw)


def lat_ctx_tiles(nlat=NLAT, nctx=NCTX, tile=512):
    t = [(c0, tile, 0) for c0 in range(0, nlat, tile)]
    if nctx:
        t.append((nlat, nctx, 1))
    return t


def mods_stage(nc, P, cT, ada_w, ada_bT, gT, vec_out):
  with ExitStack() as es:
    NW = 1152
    NB = 3
    sT = sb(nc, es, "m_sT", [128, KC, 2], F32)
    cin = sb(nc, es, "m_cin", [128, KC, 2], F32)
    bT = sb(nc, es, "m_bT", [128, DEPTH, 72], F32)
    gs = sb(nc, es, "m_gs", [128, DEPTH, 3, KC], F32)
    modt = sb(nc, es, "m_mod", [128, 72, 2], F32)
    vec = sb(nc, es, "m_vec", [128, DEPTH, 9, 2, KC], F32)
    wb = [sb(nc, es, "m_w%d" % i, [128, KC, NW], F32) for i in range(NB)]
    ps = [pp(nc, es, "m_ps%d" % i, [128, 72, 2], F32) for i in range(2)]
    b_c, b_s, b_b, b_g, b_mod, b_vec = (Buf() for _ in range(6))
    b_w = [Buf() for _ in range(NB)]
    b_ps = [Buf() for _ in range(2)]

    P.op("sync", lambda e: e.dma_start(out=cin[:], in_=cT), writes=[b_c], dma=True)
    P.op("sync", lambda e: e.dma_start(out=bT[:], in_=ada_bT), writes=[b_b], dma=True)
    P.op("sync", lambda e: e.dma_start(out=gs[:], in_=gT), writes=[b_g], dma=True)
    P.op("scalar", _act(sT[:], cin[:], AF.Silu), reads=[b_c], writes=[b_s])
    wv = ada_w.rearrange("l (k p) n -> l p k n", p=128)
    li = 0
    for l in range(DEPTH):
        pst = ps[l % 2]
        for blk in range(9216 // NW):
            slot = li % NB
            q = "sync" if li % 2 == 0 else "gpsimd"
            P.op(q, (lambda e, l=l, blk=blk, slot=slot: e.dma_start(
                out=wb[slot][:], in_=wv[l, :, :, blk * NW:(blk + 1) * NW])),
                writes=[b_w[slot]], dma=True)
            li += 1
            for nn in range(NW // 128):
                n = blk * (NW // 128) + nn
                for k in range(KC):
                    P.op("tensor", (lambda e, pst=pst, n=n, slot=slot, nn=nn, k=k: e.matmul(
                        pst[:, n, :], lhsT=wb[slot][:, k, nn * 128:(nn + 1) * 128], rhs=sT[:, k, :],
                        start=(k == 0), stop=(k == KC - 1))),
                        reads=[b_w[slot], b_s], writes=[b_ps[l % 2]])
        for s in range(2):
            P.op("vector", (lambda e, pst=pst, l=l, s=s: e.tensor_tensor(
                out=modt[:, :, s], in0=pst[:, :, s], in1=bT[:, l, :], op=ALU.add)),
                reads=[b_ps[l % 2], b_b], writes=[b_mod])
        for u in range(3):
            for s in range(2):
                P.op("vector", (lambda e, l=l, u=u, s=s: e.scalar_tensor_tensor(
                    out=vec[:, l, 3 * u, s, :], in0=modt[:, (3 * u + 1) * 8:(3 * u + 2) * 8, s], scalar=1.0,
                    in1=gs[:, l, u, :], op0=ALU.add, op1=ALU.mult)),
                    reads=[b_mod, b_g], writes=[b_vec])
                P.op("vector", (lambda e, l=l, u=u, s=s: e.tensor_copy(
                    out=vec[:, l, 3 * u + 1, s, :], in_=modt[:, (3 * u) * 8:(3 * u + 1) * 8, s])),
                    reads=[b_mod], writes=[b_vec])
                P.op("vector", (lambda e, l=l, u=u, s=s: e.tensor_scalar(
                    out=vec[:, l, 3 * u + 2, s, :], in0=modt[:, (3 * u + 2) * 8:(3 * u + 3) * 8, s],
                    scalar1=(1.0 if u == 1 else 0.5), scalar2=None, op0=ALU.mult)),
                    reads=[b_mod], writes=[b_vec])
    P.op("sync", lambda e: e.dma_start(out=vec_out, in_=vec[:]), reads=[b_vec], dma=True)
    P.flush()


class Consts:
    def __init__(self, nc, P, es):
        self.ones_bf = sb(nc, es, "c_ones", [128, 128], BF16)
        self.b_ones = Buf()
        P.op("vector", lambda e: e.memset(self.ones_bf[:], 1.0), writes=[self.b_ones])


def emit_norm_mod(nc, P, C, xt, b_x, n, sq, b_sq, ss_ps, b_ss, rstd, b_rstd, xr, b_xr, hb, b_hb,
                  A, Bv, b_vec, mod_eng="gpsimd"):
    P.op("scalar", _act(sq[:, :, :n], xt[:, :, :n], AF.Square), reads=[b_x], writes=[b_sq])
    for k in range(KC):
        P.op("tensor", (lambda e, k=k: e.matmul(ss_ps[:, :n], lhsT=C.ones_bf[:], rhs=sq[:, k, :n],
                                               start=(k == 0), stop=(k == KC - 1))),
             reads=[b_sq, C.b_ones], writes=[b_ss])
    P.op("scalar", _act(rstd[:, :n], ss_ps[:, :n], AF.Sqrt, bias=EPS, scale=1.0 / D),
         reads=[b_ss], writes=[b_rstd])
    P.op("vector", lambda e: e.reciprocal(out=rstd[:, :n], in_=rstd[:, :n]), reads=[b_rstd], writes=[b_rstd])
    for k in range(KC):
        j = k % 2
        P.op("vector", (lambda e, k=k, j=j: e.tensor_tensor(out=xr[j][:, :n], in0=xt[:, k, :n], in1=rstd[:, :n],
                                                         op=ALU.mult)),
             reads=[b_x, b_rstd], writes=[b_xr[j]])
        if mod_eng == "scalar":
            P.op("scalar", _act(hb[:, k, :n], xr[j][:, :n], AF.Identity, bias=Bv[:, k:k + 1], scale=A[:, k:k + 1]),
                 reads=[b_xr[j], b_vec], writes=[b_hb])
        else:
            P.op("gpsimd", (lambda e, k=k, j=j: e.tensor_scalar(out=hb[:, k, :n], in0=xr[j][:, :n],
                                                             scalar1=A[:, k:k + 1], scalar2=Bv[:, k:k + 1],
                                                             op0=ALU.mult, op1=ALU.add)),
                 reads=[b_xr[j], b_vec], writes=[b_hb])


def ffn_stage(nc, P, xin, xout, w_in, w_out, vec_d, layer, u, tiles, final_g=None):
    with ExitStack() as es:
        C = Consts(nc, P, es)
        NT = 512
        wi = sb(nc, es, "f_wi", [128, KC, 2 * DFF], BF16)
        wo = sb(nc, es, "f_wo", [128, FC, D], BF16)
        xt = [sb(nc, es, "f_xt%d" % i, [128, KC, NT], F32) for i in range(2)]
        hb = sb(nc, es, "f_hb", [128, KC, NT], BF16)
        ab = sb(nc, es, "f_ab", [128, FC, NT], BF16)
        rstd = sb(nc, es, "f_rstd", [128, NT], F32)
        xr = [sb(nc, es, "f_xr%d" % i, [128, NT], F32) for i in range(2)]
        sg = [sb(nc, es, "f_sg%d" % i, [128, NT], F32) for i in range(2)]
        vec = sb(nc, es, "f_vec", [128, 3, 2, KC], F32)
        gps = [pp(nc, es, "f_g%d" % i, [128, NT], F32) for i in range(2)]
        ups = [pp(nc, es, "f_u%d" % i, [128, NT], F32) for i in range(2)]
        yps = [pp(nc, es, "f_y%d" % i, [128, NT], F32) for i in range(2)]
        ssp = pp(nc, es, "f_ss", [128, NT], F32)
        b_xt = [Buf(), Buf()]
        b_hb, b_rstd, b_vec, b_ss = Buf(), Buf(), Buf(), Buf()
        b_ab = [Buf() for _ in range(FC)]
        b_xr = [Buf(), Buf()]
        b_sg = [Buf(), Buf()]
        b_g = [Buf(), Buf()]
        b_u = [Buf(), Buf()]
        b_y = [Buf(), Buf()]

        P.op("sync", lambda e: e.dma_start(out=vec[:], in_=vec_d[:, layer, 3 * u:3 * u + 3, :, :]),
             writes=[b_vec], dma=True)
        if final_g is not None:
            gf = sb(nc, es, "f_gf", [128, KC], F32)
            b_gf = Buf()
            P.op("sync", lambda e: e.dma_start(out=gf[:], in_=final_g), writes=[b_gf], dma=True)
        wiv = w_in.rearrange("(k p) n -> p k n", p=128)
        wov = w_out.rearrange("(f p) n -> p f n", p=128)
        CB = 512
        nblk = (DFF + CB - 1) // CB
        b_wi = {}
        b_wo = {}
        for blk in range(nblk):
            c0 = blk * CB
            c1 = min(DFF, c0 + CB)
            for half in range(2):
                bb = Buf()
                b_wi[(half, blk)] = bb
                o = half * DFF
                P.op("gpsimd", (lambda e, o=o, c0=c0, c1=c1: e.dma_start(
                    out=wi[:, :, o + c0:o + c1], in_=wiv[:, :, o + c0:o + c1])), writes=[bb], dma=True)
        for fg in range(0, FC, 4):
            f1 = min(FC, fg + 4)
            bb = Buf()
            for f in range(fg, f1):
                b_wo[f] = bb
            P.op("gpsimd", (lambda e, fg=fg, f1=f1: e.dma_start(out=wo[:, fg:f1, :], in_=wov[:, fg:f1, :])),
                 writes=[bb], dma=True)

        xiv = xin.rearrange("(k p) t -> p k t", p=128)
        xov = xout.rearrange("(k p) t -> p k t", p=128)

        def load(ti):
            c0, n, s = tiles[ti]
            slot = ti % 2
            P.op("sync", (lambda e: e.dma_start(out=xt[slot][:, :, :n], in_=xiv[:, :, c0:c0 + n])),
                 writes=[b_xt[slot]], dma=True)

        def norm(ti):
            c0, n, s = tiles[ti]
            slot = ti % 2
            emit_norm_mod(nc, P, C, xt[slot], b_xt[slot], n, hb, b_hb, ssp, b_ss, rstd, b_rstd, xr, b_xr,
                          hb, b_hb, vec[:, 0, s, :], vec[:, 1, s, :], b_vec)

        def phase1(ti):
            c0, n, s = tiles[ti]
            for f in range(FC):
                j = f % 2
                blk = (f * 128) // CB
                for half, pst, bp in ((0, gps[j], b_g[j]), (1, ups[j], b_u[j])):
                    o = half * DFF + f * 128
                    for k in range(KC):
                        P.op("tensor", (lambda e, pst=pst, o=o, k=k: e.matmul(
                            pst[:, :n], lhsT=wi[:, k, o:o + 128], rhs=hb[:, k, :n],
                            start=(k == 0), stop=(k == KC - 1))),
                            reads=[b_wi[(half, blk)], b_hb], writes=[bp])
                P.op("scalar", _act(sg[j][:, :n], gps[j][:, :n], AF.Silu), reads=[b_g[j]], writes=[b_sg[j]])
                P.op("vector", (lambda e, j=j, f=f: e.tensor_tensor(out=ab[:, f, :n], in0=ups[j][:, :n],
                                                                 in1=sg[j][:, :n], op=ALU.mult)),
                     reads=[b_u[j], b_sg[j]], writes=[b_ab[f]])

        def phase2(ti):
            c0, n, s = tiles[ti]
            slot = ti % 2
            for d in range(KC):
                j = d % 2
                for f in range(FC):
                    P.op("tensor", (lambda e, j=j, f=f, d=d: e.matmul(
                        yps[j][:, :n], lhsT=wo[:, f, d * 128:(d + 1) * 128], rhs=ab[:, f, :n],
                        start=(f == 0), stop=(f == FC - 1))),
                        reads=[b_wo[f], b_ab[f]], writes=[b_y[j]])
                P.op("vector", (lambda e, j=j, d=d: e.scalar_tensor_tensor(
                    out=xt[slot][:, d, :n], in0=yps[j][:, :n], scalar=vec[:, 2, s, d:d + 1],
                    in1=xt[slot][:, d, :n], op0=ALU.mult, op1=ALU.add)),
                    reads=[b_y[j], b_vec, b_xt[slot]], writes=[b_xt[slot]])
            if final_g is not None:
                x = xt[slot]
                P.op("scalar", _act(ab[:, 0:KC, :n], x[:, :, :n], AF.Square), reads=[b_xt[slot]], writes=b_ab[0:KC])
                for k in range(KC):
                    P.op("tensor", (lambda e, k=k: e.matmul(ssp[:, :n], lhsT=C.ones_bf[:], rhs=ab[:, k, :n],
                                                            start=(k == 0), stop=(k == KC - 1))),
                         reads=[b_ab[k], C.b_ones], writes=[b_ss])
                P.op("scalar", _act(rstd[:, :n], ssp[:, :n], AF.Sqrt, bias=EPS, scale=1.0 / D),
                     reads=[b_ss], writes=[b_rstd])
                P.op("vector", (lambda e: e.reciprocal(out=rstd[:, :n], in_=rstd[:, :n])), reads=[b_rstd], writes=[b_rstd])
                for k in range(KC):
                    P.op("vector", (lambda e, k=k: e.scalar_tensor_tensor(
                        out=x[:, k, :n], in0=x[:, k, :n], scalar=gf[:, k:k + 1], in1=rstd[:, :n],
                        op0=ALU.mult, op1=ALU.mult)),
                        reads=[b_xt[slot], b_rstd, b_gf], writes=[b_xt[slot]])
            P.op("sync", (lambda e: e.dma_start(out=xov[:, :, c0:c0 + n], in_=xt[slot][:, :, :n])),
                 reads=[b_xt[slot]], dma=True)

        nt = len(tiles)
        load(0)
        if nt > 1:
            load(1)
        norm(0)
        for ti in range(nt):
            phase1(ti)
            if ti + 1 < nt:
                norm(ti + 1)
            phase2(ti)
            if ti + 2 < nt:
                load(ti + 2)
        P.flush()


def final_norm_stage(nc, P, xin, xout, gfT, tiles):
    with ExitStack() as es:
        C = Consts(nc, P, es)
        NT = 512
        xt = [sb(nc, es, "n_xt%d" % i, [128, KC, NT], F32) for i in range(2)]
        sq = sb(nc, es, "n_sq", [128, KC, NT], BF16)
        rstd = sb(nc, es, "n_rstd", [128, NT], F32)
        gf = sb(nc, es, "n_gf", [128, KC], F32)
        ssp = pp(nc, es, "n_ss", [128, NT], F32)
        b_xt = [Buf(), Buf()]
        b_sq, b_rstd, b_gf, b_ss = Buf(), Buf(), Buf(), Buf()
        P.op("sync", lambda e: e.dma_start(out=gf[:], in_=gfT), writes=[b_gf], dma=True)
        xiv = xin.rearrange("(k p) t -> p k t", p=128)
        xov = xout.rearrange("(k p) t -> p k t", p=128)
        for ti, (c0, n, s) in enumerate(tiles):
            slot = ti % 2
            x = xt[slot]
            P.op("sync", (lambda e, x=x, c0=c0, n=n: e.dma_start(out=x[:, :, :n], in_=xiv[:, :, c0:c0 + n])),
                 writes=[b_xt[slot]], dma=True)
            P.op("scalar", _act(sq[:, :, :n], x[:, :, :n], AF.Square), reads=[b_xt[slot]], writes=[b_sq])
            for k in range(KC):
                P.op("tensor", (lambda e, k=k, n=n: e.matmul(ssp[:, :n], lhsT=C.ones_bf[:], rhs=sq[:, k, :n],
                                                          start=(k == 0), stop=(k == KC - 1))),
                     reads=[b_sq, C.b_ones], writes=[b_ss])
            P.op("scalar", _act(rstd[:, :n], ssp[:, :n], AF.Sqrt, bias=EPS, scale=1.0 / D),
                 reads=[b_ss], writes=[b_rstd])
            P.op("vector", (lambda e, n=n: e.reciprocal(out=rstd[:, :n], in_=rstd[:, :n])),
                 reads=[b_rstd], writes=[b_rstd])
            for k in range(KC):
                P.op("vector", (lambda e, x=x, k=k, n=n: e.scalar_tensor_tensor(
                    out=x[:, k, :n], in0=x[:, k, :n], scalar=gf[:, k:k + 1], in1=rstd[:, :n],
                    op0=ALU.mult, op1=ALU.mult)),
                    reads=[b_xt[slot], b_rstd, b_gf], writes=[b_xt[slot]])
            P.op("sync", (lambda e, x=x, c0=c0, n=n: e.dma_start(out=xov[:, :, c0:c0 + n], in_=x[:, :, :n])),
                 reads=[b_xt[slot]], dma=True)
        P.flush()


POOL_W = (2, 4, 8, 16)
PT = 496
NPOOL = NLAT + 16 + NCTX + 16


def pool_tiles(with_ctx=True):
    t = []
    c = 0
    while c < NLAT:
        n = min(PT, NLAT - c)
        t.append((c, n, 0, c == 0, c + n == NLAT, c))
        c += n
    if with_ctx:
        t.append((NLAT, NCTX, 1, True, True, NLAT))
    return t


def pool_stage(nc, P, xin, g16, xout, pool_w, pool_scT, edge_d, hmask_d, vec_d, layer, with_ctx=True):
    with ExitStack() as es:
        C = Consts(nc, P, es)
        NB = 512
        xt = [sb(nc, es, "p_xt%d" % i, [128, KC, NB], F32) for i in range(3)]
        sq = sb(nc, es, "p_sq", [128, KC, NB], BF16)
        hf = [sb(nc, es, "p_hf%d" % i, [128, KC, NB], F32) for i in range(2)]
        sA = sb(nc, es, "p_sA", [128, 2, NB], F32)
        sB = sb(nc, es, "p_sB", [128, 2, NB], F32)
        sC = sb(nc, es, "p_sC", [128, 2, NB], F32)
        sD = sb(nc, es, "p_sD", [128, 2, NB], F32)
        b_sC, b_sD = Buf(), Buf()
        tmp = sb(nc, es, "p_tmp", [128, 2, 8], F32)
        df = sb(nc, es, "p_df", [128, KC, NB], BF16)
        rstd = sb(nc, es, "p_rstd", [128, NB], F32)
        xr = [sb(nc, es, "p_xr%d" % i, [128, NB], F32) for i in range(2)]
        vec = sb(nc, es, "p_vec", [128, 3, 2, KC], F32)
        psc = sb(nc, es, "p_psc", [128, KC], F32)
        psg = sb(nc, es, "p_psg", [128, 2, KC], F32)
        edge = sb(nc, es, "p_edge", [128, 4, KC, 8], F32)
        hmask = sb(nc, es, "p_hmask", [128, 2], F32)
        pw = sb(nc, es, "p_pw", [128, 4, 2, 256], BF16)
        ssp = pp(nc, es, "p_ss", [128, NB], F32)
        yps = [pp(nc, es, "p_y%d" % i, [128, NB], F32) for i in range(2)]
        b_xt = [Buf(), Buf(), Buf()]
        b_hf = [Buf(), Buf()]
        b_sq, b_sA, b_sB, b_tmp, b_df, b_rstd, b_vec, b_psc, b_psg, b_edge, b_hm, b_pw, b_ss = (
            Buf() for _ in range(13))
        b_xr = [Buf(), Buf()]
        b_y = [Buf(), Buf()]

        P.op("sync", lambda e: e.dma_start(out=vec[:], in_=vec_d[:, layer, 3:6, :, :]), writes=[b_vec], dma=True)
        P.op("sync", lambda e: e.dma_start(out=psc[:], in_=pool_scT), writes=[b_psc], dma=True)
        P.op("sync", lambda e: e.dma_start(out=edge[:], in_=edge_d), writes=[b_edge], dma=True)
        P.op("sync", lambda e: e.dma_start(out=hmask[:], in_=hmask_d), writes=[b_hm], dma=True)
        P.op("gpsimd", lambda e: e.dma_start(out=pw[:], in_=pool_w.rearrange("g (c p) e -> p g c e", p=128)),
             writes=[b_pw], dma=True)
        for s in range(2):
            P.op("vector", (lambda e, s=s: e.tensor_tensor(out=psg[:, s, :], in0=vec[:, 2, s, :], in1=psc[:],
                                                         op=ALU.mult)),
                 reads=[b_vec, b_psc], writes=[b_psg])

        xiv = xin.rearrange("(k p) t -> p k t", p=128)
        g16v = g16.rearrange("(r k p) t -> r p k t", r=2, p=128)
        xov = xout.rearrange("(k p) t -> p k t", p=128)
        tiles = pool_tiles(with_ctx)

        def norm_part(ti):
            c0, n, s, ledge, redge, oc0 = tiles[ti]
            slot = ti % 3
            x = xt[slot]
            hfc, b_hfc = hf[ti % 2], b_hf[ti % 2]
            nb = n + 16
            lo = 8 if ledge else 0
            hi = 8 + n if redge else nb
            P.op("sync", (lambda e: e.dma_start(out=x[:, :, lo:hi], in_=xiv[:, :, c0 - 8 + lo:c0 - 8 + hi])),
                 writes=[b_xt[slot]], dma=True)
            if s == 1:
                P.op("gpsimd", (lambda e: e.memset(x[:, :, 0:8], 0.0)), writes=[b_xt[slot]])
                P.op("gpsimd", (lambda e: e.memset(x[:, :, 8 + n:16 + n], 0.0)), writes=[b_xt[slot]])
            else:
                if ledge:
                    P.op("sync", (lambda e: e.dma_start(out=x[:, :, 0:8], in_=g16v[0, :, :, 8:16])),
                         writes=[b_xt[slot]], dma=True)
                if redge:
                    P.op("sync", (lambda e: e.dma_start(out=x[:, :, 8 + n:16 + n], in_=g16v[1, :, :, 0:8])),
                         writes=[b_xt[slot]], dma=True)
            emit_norm_mod(nc, P, C, x, b_xt[slot], nb, sq, b_sq, ssp, b_ss, rstd, b_rstd, xr, b_xr,
                          hfc, b_hfc, vec[:, 0, s, :], vec[:, 1, s, :], b_vec, mod_eng="scalar")
            if s == 1:
                P.op("gpsimd", lambda e: e.memset(hfc[:, :, 0:8], 0.0), writes=[b_hfc])
                P.op("gpsimd", (lambda e: e.memset(hfc[:, :, 8 + n:16 + n], 0.0)), writes=[b_hfc])
            else:
                if ledge:
                    P.op("gpsimd", lambda e: e.tensor_scalar(out=hfc[:, :, 0:8], in0=hfc[:, :, 0:8],
                                                             scalar1=hmask[:, 0:1], scalar2=0.0,
                                                             op0=ALU.mult, op1=ALU.add),
                         reads=[b_hm], writes=[b_hfc])
                if redge:
                    P.op("gpsimd", (lambda e: e.tensor_scalar(out=hfc[:, :, 8 + n:16 + n],
                                                              in0=hfc[:, :, 8 + n:16 + n],
                                                              scalar1=hmask[:, 1:2], scalar2=0.0,
                                                              op0=ALU.mult, op1=ALU.add)),
                         reads=[b_hm], writes=[b_hfc])

        def core_part(ti):
            c0, n, s, ledge, redge, oc0 = tiles[ti]
            slot = ti % 3
            x = xt[slot]
            hfc, b_hfc = hf[ti % 2], b_hf[ti % 2]
            nb = n + 16
            for gi, w in enumerate(POOL_W):
                hv = hfc[:, 2 * gi:2 * gi + 2, :]
                cur, bcur, ln = hv, b_hfc, nb
                step = 1
                bufs = [(sA, b_sA), (sB, b_sB)] if gi >= 2 else [(sC, b_sC), (sD, b_sD)]
                aeng = "gpsimd" if gi >= 2 else "vector"
                bi = 0
                while step < w:
                    dst, bdst = bufs[bi]
                    bi ^= 1
                    nl = ln - step
                    P.op(aeng, (lambda e, dst=dst, cur=cur, nl=nl, step=step: e.tensor_tensor(
                        out=dst[:, :, 0:nl], in0=cur[:, :, 0:nl], in1=cur[:, :, step:step + nl], op=ALU.add)),
                        reads=[bcur], writes=[bdst])
                    cur, bcur, ln = dst, bdst, nl
                    step *= 2
                o = 8 - w // 2
                P.op("vector", (lambda e, cur=cur, o=o, w=w, gi=gi: e.scalar_tensor_tensor(
                    out=df[:, 2 * gi:2 * gi + 2, :n], in0=cur[:, :, o:o + n], scalar=1.0 / w,
                    in1=hfc[:, 2 * gi:2 * gi + 2, 8:8 + n], op0=ALU.mult, op1=ALU.subtract)),
                    reads=[bcur, b_hfc], writes=[b_df])
                fixes = []
                if ledge:
                    fixes.append((0, 0 if s == 0 else 2))
                if redge:
                    fixes.append((n - 8, 1 if s == 0 else 3))
                for (q0, tb) in fixes:
                    P.op("vector", (lambda e, cur=cur, o=o, q0=q0, tb=tb, gi=gi: e.tensor_tensor(
                        out=tmp[:], in0=cur[:, :, o + q0:o + q0 + 8], in1=edge[:, tb, 2 * gi:2 * gi + 2, :],
                        op=ALU.mult)), reads=[bcur, b_edge], writes=[b_tmp])
                    P.op("vector", (lambda e, q0=q0, gi=gi: e.tensor_tensor(
                        out=df[:, 2 * gi:2 * gi + 2, q0:q0 + 8], in0=tmp[:],
                        in1=hfc[:, 2 * gi:2 * gi + 2, 8 + q0:16 + q0], op=ALU.subtract)),
                        reads=[b_tmp, b_hfc], writes=[b_df])
            for gi in range(4):
                for ec in range(2):
                    d = 2 * gi + ec
                    j = d % 2
                    for cc in range(2):
                        P.op("tensor", (lambda e, j=j, gi=gi, ec=ec, cc=cc: e.matmul(
                            yps[j][:, :n], lhsT=pw[:, gi, cc, ec * 128:(ec + 1) * 128], rhs=df[:, 2 * gi + cc, :n],
                            start=(cc == 0), stop=(cc == 1))),
                            reads=[b_pw, b_df], writes=[b_y[j]])
                    P.op("vector", (lambda e, j=j, d=d: e.scalar_tensor_tensor(
                        out=x[:, d, 8:8 + n], in0=yps[j][:, :n], scalar=psg[:, s, d:d + 1],
                        in1=x[:, d, 8:8 + n], op0=ALU.mult, op1=ALU.add)),
                        reads=[b_y[j], b_psg, b_xt[slot]], writes=[b_xt[slot]])
            P.op("sync", (lambda e: e.dma_start(out=xov[:, :, oc0:oc0 + n], in_=x[:, :, 8:8 + n])),
                 reads=[b_xt[slot]], dma=True)

        norm_part(0)
        for ti in range(len(tiles)):
            if ti + 1 < len(tiles):
                norm_part(ti + 1)
            core_part(ti)
        P.flush()


def fm(v):
    v = np.asarray(v, np.float32)
    lead = v.shape[:-1]
    a = v.reshape(lead + (KC, 128))
    return np.ascontiguousarray(np.moveaxis(a, -1, 0))


def pool_edge_tables(half):
    n = 2 * NLAT
    e = np.zeros((128, 4, KC, 8), np.float32)
    for k in range(KC):
        w = POOL_W[k // 2]
        for i in range(8):
            t = i
            cl = min(t, w // 2) + w // 2
            t = n - 8 + i
            cr = min(t + w // 2 - 1, n - 1) - (t - w // 2) + 1
            e[:, 0, k, i] = 1.0 / cl if half == 0 else 1.0 / w
            e[:, 1, k, i] = 1.0 / cr if half == 1 else 1.0 / w
            t = i
            e[:, 2, k, i] = 1.0 / (min(t, w // 2) + w // 2)
            t = NCTX - 8 + i
            e[:, 3, k, i] = 1.0 / (min(t + w // 2 - 1, NCTX - 1) - (t - w // 2) + 1)
    return e


def pool_halo_layout(xs, c):
    half = c % 2
    me, partner = xs[c], xs[c ^ 1]
    z = np.zeros((D, 8), np.float32)
    left = z if half == 0 else partner[:, NLAT - 8:NLAT]
    right = partner[:, 0:8] if half == 0 else z
    return np.ascontiguousarray(np.concatenate([left, me[:, :NLAT], right, z, me[:, NLAT:], z], axis=1))


def qkv_stage(nc, P, xin, w_qkv, bqkT, vec_d, layer, cosT, snT, q_scale, nblk, qT, kTl, kTc, vl, vc, stats_d, tiles,
              gather=None):
    rope = cosT is not None
    with ExitStack() as es:
        C = Consts(nc, P, es)
        NT = 512
        w = sb(nc, es, "q_w", [128, KC, 3 * D], BF16)
        xt = [sb(nc, es, "q_xt%d" % i, [128, KC, NT], F32) for i in range(2)]
        hb = sb(nc, es, "q_hb", [128, KC, NT], BF16)
        rstd = sb(nc, es, "q_rstd", [128, NT], F32)
        xr = [sb(nc, es, "q_xr%d" % i, [128, NT], F32) for i in range(2)]
        t1 = [sb(nc, es, "q_t1%d" % i, [128, NT], F32) for i in range(2)]
        t2 = [sb(nc, es, "q_t2%d" % i, [128, NT], F32) for i in range(2)]
        ob = [sb(nc, es, "q_ob%d" % i, [128, NT], BF16) for i in range(3)]
        sqb = sb(nc, es, "q_sqb", [128, NT], BF16)
        vb = [sb(nc, es, "q_vb%d" % i, [128, NT], BF16) for i in range(2)]
        vec = sb(nc, es, "q_vec", [128, 3, 2, KC], F32)
        bq = sb(nc, es, "q_bq", [128, 16], F32)
        stats = sb(nc, es, "q_stats", [128, 16], F32)
        mtmp = sb(nc, es, "q_mtmp", [128, 1], F32)
        blk = sb(nc, es, "q_blk", [128, 128], BF16)
        psA = [pp(nc, es, "q_pa%d" % i, [128, NT], F32) for i in range(2)]
        psB = [pp(nc, es, "q_pb%d" % i, [128, NT], F32) for i in range(2)] if rope else None
        psv = [pp(nc, es, "q_pv%d" % i, [128, NT], F32) for i in range(2)]
        ssp = pp(nc, es, "q_ss", [128, NT], F32)
        nsp = pp(nc, es, "q_ns", [128, NT], F32)
        b_w = [Buf() for _ in range(6)]
        b_xt = [Buf(), Buf()]
        b_hb, b_rstd, b_vec, b_bq, b_stats, b_mtmp, b_blk, b_ss, b_ns, b_sqb, b_wp, b_cs = (Buf() for _ in range(12))
        b_xr = [Buf(), Buf()]
        b_t1 = [Buf(), Buf()]
        b_t2 = [Buf(), Buf()]
        b_ob = [Buf() for _ in range(3)]
        b_vb = [Buf(), Buf()]
        b_pa = [Buf(), Buf()]
        b_pb = [Buf(), Buf()]
        b_pv = [Buf(), Buf()]

        P.op("sync", lambda e: e.dma_start(out=vec[:], in_=vec_d[:, layer, 3:6, :, :]), writes=[b_vec], dma=True)
        P.op("sync", lambda e: e.dma_start(out=bq[:], in_=bqkT), writes=[b_bq], dma=True)
        P.op("vector", lambda e: e.memset(stats[:], 0.0), writes=[b_stats])
        P.op("vector", lambda e: e.memset(blk[:], 0.0), writes=[b_blk])
        bs = 128 // nblk
        for i in range(nblk):
            P.op("vector", (lambda e, i=i: e.memset(blk[i * bs:(i + 1) * bs, i * bs:(i + 1) * bs], 1.0)),
                 writes=[b_blk])
        wv_ = w_qkv.rearrange("(k p) n -> p k n", p=128)
        for i in range(6):
            P.op("gpsimd", (lambda e, i=i: e.dma_start(out=w[:, :, i * 512:(i + 1) * 512],
                                                        in_=wv_[:, :, i * 512:(i + 1) * 512])),
                 writes=[b_w[i]], dma=True)
        if rope:
            wp = sb(nc, es, "q_wp", [128, KC, 2 * D], BF16)
            cs = sb(nc, es, "q_cs", [128, 2, NLAT], F32)
            P.op("sync", lambda e: e.dma_start(out=cs[:, 0, :], in_=cosT), writes=[b_cs], dma=True)
            P.op("sync", lambda e: e.dma_start(out=cs[:, 1, :], in_=snT), writes=[b_cs], dma=True)
            for k in range(KC):
                src = w[:, k, 0:2 * D].rearrange("p (b h s) -> p b h s", h=2, s=16)
                dst = wp[:, k, :].rearrange("p (b h s) -> p b h s", h=2, s=16)
                for hh in range(2):
                    eng = "gpsimd" if (2 * k + hh) % 2 == 0 else "scalar"
                    if eng == "gpsimd":
                        P.op(eng, (lambda e, dst=dst, src=src, hh=hh: e.tensor_copy(out=dst[:, :, hh, :],
                                                                                in_=src[:, :, 1 - hh, :])),
                             reads=b_w[0:4], writes=[b_wp])
                    else:
                        P.op(eng, (lambda e, dst=dst, src=src, hh=hh: e.activation(out=dst[:, :, hh, :],
                                                                                in_=src[:, :, 1 - hh, :],
                                                                                func=AF.Copy)),
                             reads=b_w[0:4], writes=[b_wp])

        xiv = xin.rearrange("(k p) t -> p k t", p=128)
        ci = 0
        vi = 0
        deferred = []
        gpending = []
        for ti, (c0, n, s) in enumerate(tiles):
            slot = ti % 2
            x = xt[slot]
            P.op("sync", (lambda e, x=x, c0=c0, n=n: e.dma_start(out=x[:, :, :n], in_=xiv[:, :, c0:c0 + n])),
                 writes=[b_xt[slot]], dma=True)
            emit_norm_mod(nc, P, C, x, b_xt[slot], n, hb, b_hb, ssp, b_ss, rstd, b_rstd, xr, b_xr,
                          hb, b_hb, vec[:, 0, s, :], vec[:, 1, s, :], b_vec)
            while gpending:
                gpending.pop(0)()
            dorope = rope and s == 0
            b_kd, b_vd = Buf(), Buf()
            for c in range(16):
                j = ci % 2
                oj = ci % 3
                ci += 1
                for k in range(KC):
                    P.op("tensor", (lambda e, j=j, c=c, k=k, n=n: e.matmul(
                        psA[j][:, :n], lhsT=w[:, k, c * 128:(c + 1) * 128], rhs=hb[:, k, :n],
                        start=(k == 0), stop=(k == KC - 1))),
                        reads=[b_w[c // 4], b_hb], writes=[b_pa[j]])
                if dorope:
                    for k in range(KC):
                        P.op("tensor", (lambda e, j=j, c=c, k=k, n=n: e.matmul(
                            psB[j][:, :n], lhsT=wp[:, k, c * 128:(c + 1) * 128], rhs=hb[:, k, :n],
                            start=(k == 0), stop=(k == KC - 1))),
                            reads=[b_wp, b_hb], writes=[b_pb[j]])
                    P.op("vector", (lambda e, j=j, c0=c0, n=n: e.tensor_tensor(
                        out=t1[j][:, :n], in0=psA[j][:, :n], in1=cs[:, 0, c0:c0 + n], op=ALU.mult)),
                        reads=[b_pa[j], b_cs], writes=[b_t1[j]])
                    P.op("vector", (lambda e, j=j, c0=c0, n=n: e.tensor_tensor(
                        out=t2[j][:, :n], in0=psB[j][:, :n], in1=cs[:, 1, c0:c0 + n], op=ALU.mult)),
                        reads=[b_pb[j], b_cs], writes=[b_t2[j]])
                    P.op("gpsimd", (lambda e, j=j, oj=oj, n=n: e.tensor_tensor(
                        out=ob[oj][:, :n], in0=t1[j][:, :n], in1=t2[j][:, :n], op=ALU.add)),
                        reads=[b_t1[j], b_t2[j]], writes=[b_ob[oj]])
                else:
                    sc = q_scale if c < 8 else 1.0
                    P.op("vector", (lambda e, j=j, oj=oj, c=c, n=n, sc=sc: e.tensor_scalar(
                        out=ob[oj][:, :n], in0=psA[j][:, :n], scalar1=bq[:, c:c + 1], scalar2=sc,
                        op0=ALU.add, op1=ALU.mult)),
                        reads=[b_pa[j], b_bq], writes=[b_ob[oj]])
                if c < 8:
                    dst = qT[c * 128:(c + 1) * 128, c0:c0 + n]
                elif s == 0 and isinstance(kTl, list):
                    dst = kTl[ti][(c - 8) * 128:(c - 7) * 128, 0:n]
                elif s == 0:
                    dst = kTl[(c - 8) * 128:(c - 7) * 128, c0:c0 + n]
                else:
                    dst = kTc[(c - 8) * 128:(c - 7) * 128, 0:n]
                wr = [b_kd] if (gather is not None and s == 0 and c >= 8) else []
                P.op("sync", (lambda e, dst=dst, oj=oj, n=n: e.dma_start(out=dst, in_=ob[oj][:, :n])),
                     reads=[b_ob[oj]], writes=wr, dma=True)
                def stats_ops(oj=oj, n=n, c=c):
                    P.op("scalar", _act(sqb[:, :n], ob[oj][:, :n], AF.Square), reads=[b_ob[oj]], writes=[b_sqb])
                    P.op("tensor", (lambda e: e.matmul(nsp[:, :n], lhsT=blk[:], rhs=sqb[:, :n], start=True, stop=True)),
                         reads=[b_blk, b_sqb], writes=[b_ns])
                    P.op("vector", (lambda e: e.tensor_reduce(out=mtmp[:], in_=nsp[:, :n],
                                                              axis=mybir.AxisListType.X, op=ALU.max)),
                         reads=[b_ns], writes=[b_mtmp])
                    P.op("vector", (lambda e: e.tensor_tensor(out=stats[:, c:c + 1], in0=stats[:, c:c + 1],
                                                             in1=mtmp[:], op=ALU.max)),
                         reads=[b_mtmp, b_stats], writes=[b_stats])
                if deferred:
                    deferred.pop(0)()
                deferred.append(stats_ops)
            while deferred:
                deferred.pop(0)()
            for sub in range(n // 128):
                for vh in range(2):
                    j = vi % 2
                    vi += 1
                    for k in range(KC):
                        P.op("tensor", (lambda e, j=j, k=k, sub=sub, vh=vh: e.matmul(
                            psv[j][:, :], lhsT=hb[:, k, sub * 128:(sub + 1) * 128],
                            rhs=w[:, k, 2 * D + vh * 512:2 * D + (vh + 1) * 512],
                            start=(k == 0), stop=(k == KC - 1))),
                            reads=[b_w[4 + vh], b_hb], writes=[b_pv[j]])
                    P.op("scalar", _act(vb[j][:], psv[j][:], AF.Copy), reads=[b_pv[j]], writes=[b_vb[j]])
                    if s == 0 and isinstance(vl, list):
                        vdst, r0 = vl[ti], sub * 128
                    else:
                        vdst = vl if s == 0 else vc
                        r0 = (c0 if s == 0 else 0) + sub * 128
                    wr = [b_vd] if (gather is not None and s == 0) else []
                    P.op("sync", (lambda e, j=j, r0=r0, vh=vh, vdst=vdst: e.dma_start(
                        out=vdst[r0:r0 + 128, vh * 512:(vh + 1) * 512], in_=vb[j][:])),
                        reads=[b_vb[j]], writes=wr, dma=True)
            if gather is not None and s == 0:
                def gops(ti=ti, b_kd=b_kd, b_vd=b_vd):
                    all_gather(P, kTl[ti], gather[0][ti], [b_kd], [Buf()])
                    all_gather(P, vl[ti], gather[1][ti], [b_vd], [Buf()])
                gpending.append(gops)
        P.op("sync", lambda e: e.dma_start(out=stats_d, in_=stats[:]), reads=[b_stats], dma=True)
        P.flush()


DIFF_HEADS = 8
DIFF_SCALE = 64 ** -0.5


def diff_attn_stage(nc, P, qT, k_pieces, v_pieces, st_q, st_k0, st_k1, lamR, sublnT, lam_init, oT, nkey_lat):
    NKC_LAT = nkey_lat // 128
    NKC = NKC_LAT + NCTX // 128
    NKEY = NKC * 128
    with ExitStack() as es:
        C = Consts(nc, P, es)
        NT = 512
        kh = [sb(nc, es, "a_k%d" % i, [128, NKEY], BF16) for i in range(2)]
        vh = [sb(nc, es, "a_v%d" % i, [128, NKC, 128], BF16) for i in range(2)]
        qz = [[sb(nc, es, "a_q%d%d" % (i, j), [128, NTOK], BF16) for j in range(2)] for i in range(2)]
        pT = [sb(nc, es, "a_p%d" % i, [128, 2, NT], BF16) for i in range(3)]
        rz = [sb(nc, es, "a_rz%d" % i, [128, NT], F32) for i in range(2)]
        ta = sb(nc, es, "a_ta", [128, NT], F32)
        tb = sb(nc, es, "a_tb", [128, NT], F32)
        av = [sb(nc, es, "a_a%d" % i, [128, NT], F32) for i in range(2)]
        sqb = sb(nc, es, "a_sq", [128, NT], BF16)
        rs = sb(nc, es, "a_rs", [128, NT], F32)
        ob = [sb(nc, es, "a_ob%d" % i, [128, NT], BF16) for i in range(2)]
        sq_t = sb(nc, es, "a_stq", [128, 16], F32)
        sk0_t = sb(nc, es, "a_stk0", [128, 16], F32)
        sk1_t = sb(nc, es, "a_stk1", [128, 16], F32)
        nB = sb(nc, es, "a_nB", [128, 8], F32)
        lam_t = sb(nc, es, "a_lam", [128, 4, 64], F32)
        lp = sb(nc, es, "a_lp", [128, 2, 64], F32)
        ls = sb(nc, es, "a_ls", [128, 2], F32)
        nl = sb(nc, es, "a_nl", [128, 1], F32)
        sgs = sb(nc, es, "a_sgs", [128, 1], F32)
        sps = [pp(nc, es, "a_s%d" % i, [128, 2, NT], F32) for i in range(2)]
        pvp = [pp(nc, es, "a_pv%d" % i, [128, NT], F32) for i in range(2)]
        zp = [pp(nc, es, "a_z%d" % i, [128, NT], F32) for i in range(2)]
        ssp = zp[1]
        b_k = [Buf(), Buf()]
        b_v = [Buf(), Buf()]
        b_q = [Buf(), Buf()]
        b_p = [Buf() for _ in range(3)]
        b_rz = [Buf(), Buf()]
        b_ta, b_tb, b_sqb, b_rs, b_st, b_nB, b_lam, b_lp, b_ls, b_nl, b_sgs, b_ss = (Buf() for _ in range(12))
        b_a = [Buf(), Buf()]
        b_ob = [Buf(), Buf()]
        b_s = [Buf(), Buf()]
        b_pv = [Buf(), Buf()]
        b_z = [Buf(), Buf()]
        b_ss = b_z[1]

        for i in range(2):
            for j in range(2):
                P.op("gpsimd", (lambda e, i=i, j=j: e.memset(qz[i][j][:], 0.0)), writes=[b_q[i]])
        P.op("sync", lambda e: e.dma_start(out=sq_t[:], in_=st_q), writes=[b_st], dma=True)
        P.op("sync", lambda e: e.dma_start(out=sk0_t[:], in_=st_k0), writes=[b_st], dma=True)
        P.op("sync", lambda e: e.dma_start(out=sk1_t[:], in_=st_k1), writes=[b_st], dma=True)
        P.op("sync", lambda e: e.dma_start(out=lam_t[:], in_=lamR), writes=[b_lam], dma=True)
        P.op("sync", lambda e: e.dma_start(out=sgs[:], in_=sublnT), writes=[b_sgs], dma=True)
        P.op("vector", lambda e: e.tensor_tensor(out=sk0_t[:, 8:16], in0=sk0_t[:, 8:16], in1=sk1_t[:, 8:16],
                                                 op=ALU.max), reads=[b_st], writes=[b_st])
        P.op("vector", lambda e: e.tensor_tensor(out=nB[:], in0=sq_t[:, 0:8], in1=sk0_t[:, 8:16], op=ALU.mult),
             reads=[b_st], writes=[b_nB])
        P.op("scalar", _act(nB[:], nB[:], AF.Sqrt), reads=[b_nB], writes=[b_nB])
        P.op("vector", lambda e: e.tensor_scalar(out=nB[:], in0=nB[:], scalar1=-DIFF_SCALE, scalar2=None,
                                                 op0=ALU.mult), reads=[b_nB], writes=[b_nB])
        for i in range(2):
            P.op("vector", (lambda e, i=i: e.tensor_tensor(out=lp[:, i, :], in0=lam_t[:, 2 * i, :],
                                                         in1=lam_t[:, 2 * i + 1, :], op=ALU.mult)),
                 reads=[b_lam], writes=[b_lp])
            P.op("vector", (lambda e, i=i: e.tensor_reduce(out=ls[:, i:i + 1], in_=lp[:, i, :],
                                                         axis=mybir.AxisListType.X, op=ALU.add)),
                 reads=[b_lp], writes=[b_ls])
        P.op("scalar", _act(ls[:], ls[:], AF.Exp), reads=[b_ls], writes=[b_ls])
        P.op("vector", lambda e: e.tensor_tensor(out=nl[:], in0=ls[:, 1:2], in1=ls[:, 0:1], op=ALU.subtract),
             reads=[b_ls], writes=[b_nl])
        P.op("vector", lambda e: e.tensor_scalar(out=nl[:], in0=nl[:], scalar1=-float(lam_init), scalar2=None,
                                                 op0=ALU.add), reads=[b_nl], writes=[b_nl])
        P.op("vector", lambda e: e.tensor_scalar(out=sgs[:], in0=sgs[:], scalar1=1.0 - float(lam_init),
                                                 scalar2=None, op0=ALU.mult), reads=[b_sgs], writes=[b_sgs])

        tiles = lat_ctx_tiles()
        pending = []
        cnt = {"pi": 0, "si": 0, "ai": 0}

        def do_j(h, hs, c0, n, kcs, j):
            pr = slice(j * 64, (j + 1) * 64)

            def mm1pair(pair, sj):
                for i, kc in enumerate(pair):
                    P.op("tensor", (lambda e, i=i, kc=kc: e.matmul(
                        sps[sj][:, i, :n], lhsT=kh[hs][:, kc * 128:(kc + 1) * 128], rhs=qz[hs][j][:, c0:c0 + n],
                        start=True, stop=True)),
                        reads=[b_k[hs], b_q[hs]], writes=[b_s[sj]])

            def unit(pair, sj, pj, first, last):
                np_ = len(pair)
                P.op("scalar", _act(pT[pj][:, :np_, :n], sps[sj][:, :np_, :n], AF.Exp, bias=nB[:, h:h + 1],
                                    scale=DIFF_SCALE),
                     reads=[b_s[sj], b_nB], writes=[b_p[pj]])
                for i, kc in enumerate(pair):
                    st_ = first and i == 0
                    en_ = last and i == np_ - 1
                    P.op("tensor", (lambda e, i=i, kc=kc, st_=st_, en_=en_: e.matmul(
                        pvp[j][:, :n], lhsT=vh[hs][:, kc, :], rhs=pT[pj][:, i, :n], start=st_, stop=en_)),
                        reads=[b_v[hs], b_p[pj]], writes=[b_pv[j]])
                    P.op("tensor", (lambda e, i=i, st_=st_, en_=en_: e.matmul(
                        zp[j][:, :n], lhsT=C.ones_bf[:], rhs=pT[pj][:, i, :n], start=st_, stop=en_)),
                        reads=[C.b_ones, b_p[pj]], writes=[b_z[j]])

            pairs = [kcs[i:i + 2] for i in range(0, len(kcs), 2)]
            mm1pair(pairs[0], cnt["si"] % 2)
            for idx, pair in enumerate(pairs):
                sj = cnt["si"] % 2
                cnt["si"] += 1
                if idx + 1 < len(pairs):
                    mm1pair(pairs[idx + 1], cnt["si"] % 2)
                pj = cnt["pi"] % 3
                cnt["pi"] += 1
                unit(pair, sj, pj, idx == 0, idx == len(pairs) - 1)
                if pending and j == 0 and (idx == min(12, len(pairs) - 1)):
                    pending.pop(0)()
            P.op("vector", (lambda e: e.reciprocal(out=rz[j][:, :n], in_=zp[j][:, :n])),
                 reads=[b_z[j]], writes=[b_rz[j]])
            if j == 0:
                P.op("vector", lambda e: e.tensor_tensor(out=ta[:, :n], in0=pvp[0][:, :n], in1=rz[0][:, :n],
                                                         op=ALU.mult),
                     reads=[b_pv[0], b_rz[0]], writes=[b_ta])
            else:
                P.op("vector", lambda e: e.scalar_tensor_tensor(
                    out=tb[:, :n], in0=pvp[1][:, :n], scalar=nl[:, 0:1], in1=rz[1][:, :n],
                    op0=ALU.mult, op1=ALU.mult),
                    reads=[b_pv[1], b_rz[1], b_nl], writes=[b_tb])

        def do_tile(h, hs, c0, n, s):
            kcs = list(range(NKC)) if s == 0 else list(range(NKC_LAT, NKC))
            acur = cnt["ai"] % 2
            cnt["ai"] += 1
            for j in range(2):
                do_j(h, hs, c0, n, kcs, j)
            a = av[acur]
            P.op("gpsimd", (lambda e: e.tensor_tensor(out=a[:, :n], in0=ta[:, :n], in1=tb[:, :n], op=ALU.add)),
                 reads=[b_ta, b_tb], writes=[b_a[acur]])
            P.op("scalar", _act(sqb[:, :n], a[:, :n], AF.Square), reads=[b_a[acur]], writes=[b_sqb])

            def fin():
                P.op("tensor", (lambda e: e.matmul(ssp[:, :n], lhsT=C.ones_bf[:], rhs=sqb[:, :n],
                                                   start=True, stop=True)),
                     reads=[C.b_ones, b_sqb], writes=[b_ss])
                P.op("scalar", _act(rs[:, :n], ssp[:, :n], AF.Sqrt, bias=EPS, scale=1.0 / 128.0),
                     reads=[b_ss], writes=[b_rs])
                P.op("vector", (lambda e: e.reciprocal(out=rs[:, :n], in_=rs[:, :n])), reads=[b_rs], writes=[b_rs])
                P.op("vector", (lambda e: e.scalar_tensor_tensor(
                    out=ob[acur][:, :n], in0=a[:, :n], scalar=sgs[:, 0:1], in1=rs[:, :n],
                    op0=ALU.mult, op1=ALU.mult)),
                    reads=[b_a[acur], b_sgs, b_rs], writes=[b_ob[acur]])
                P.op("sync", (lambda e: e.dma_start(out=oT[h * 128:(h + 1) * 128, c0:c0 + n], in_=ob[acur][:, :n])),
                     reads=[b_ob[acur]], dma=True)
            pending.append(fin)

        def load_head(h):
            hs = h % 2
            col = 0
            for (kap, ncols) in k_pieces:
                P.op("sync", (lambda e, kap=kap, col=col, ncols=ncols: e.dma_start(
                    out=kh[hs][:, col:col + ncols], in_=kap[h * 128:(h + 1) * 128, :])),
                    writes=[b_k[hs]], dma=True)
                col += ncols
            for j in range(2):
                P.op("sync", (lambda e, j=j: e.dma_start(out=qz[hs][j][j * 64:(j + 1) * 64, :],
                                                         in_=qT[h * 128 + j * 64:h * 128 + (j + 1) * 64, :])),
                     writes=[b_q[hs]], dma=True)
            ch = 0
            for (vap, nrows) in v_pieces:
                nch = nrows // 128
                step = 16
                for c1 in range(0, nch, step):
                    c2 = min(nch, c1 + step)
                    vv = vap[c1 * 128:c2 * 128, h * 128:(h + 1) * 128].rearrange("(c p) e -> p c e", p=128)
                    P.op("gpsimd", (lambda e, vv=vv, a=ch + c1, b=ch + c2: e.dma_start(out=vh[hs][:, a:b, :], in_=vv)),
                         writes=[b_v[hs]], dma=True)
                ch += nch

        load_head(0)
        for h in range(DIFF_HEADS):
            if h + 1 < DIFF_HEADS:
                load_head(h + 1)
            for (c0, n, s) in tiles:
                do_tile(h, h % 2, c0, n, s)
        while pending:
            pending.pop(0)()
        P.flush()


def proj_stage(nc, P, oT, w_o, boT, xin, xout, vec_d, layer, tiles):
    with ExitStack() as es:
        NT = 512
        w = sb(nc, es, "o_w", [128, KC, D], BF16)
        ot = [sb(nc, es, "o_ot%d" % i, [128, KC, NT], BF16) for i in range(2)]
        xt = [sb(nc, es, "o_xt%d" % i, [128, KC, NT], F32) for i in range(2)]
        vec = sb(nc, es, "o_vec", [128, 2, KC], F32)
        bo = sb(nc, es, "o_bo", [128, KC], F32)
        gb = sb(nc, es, "o_gb", [128, 2, KC], F32)
        yps = [pp(nc, es, "o_y%d" % i, [128, NT], F32) for i in range(2)]
        b_w, b_vec, b_bo, b_gb = Buf(), Buf(), Buf(), Buf()
        b_ot = [Buf(), Buf()]
        b_xt = [Buf(), Buf()]
        b_y = [Buf(), Buf()]
        P.op("gpsimd", lambda e: e.dma_start(out=w[:], in_=w_o.rearrange("(k p) n -> p k n", p=128)),
             writes=[b_w], dma=True)
        P.op("sync", lambda e: e.dma_start(out=vec[:], in_=vec_d[:, layer, 5, :, :]), writes=[b_vec], dma=True)
        P.op("sync", lambda e: e.dma_start(out=bo[:], in_=boT), writes=[b_bo], dma=True)
        for s in range(2):
            P.op("vector", (lambda e, s=s: e.tensor_tensor(out=gb[:, s, :], in0=vec[:, s, :], in1=bo[:], op=ALU.mult)),
                 reads=[b_vec, b_bo], writes=[b_gb])
        xiv = xin.rearrange("(k p) t -> p k t", p=128)
        xov = xout.rearrange("(k p) t -> p k t", p=128)
        ov = oT.rearrange("(k p) t -> p k t", p=128)
        for ti, (c0, n, s) in enumerate(tiles):
            slot = ti % 2
            x = xt[slot]
            o = ot[slot]
            P.op("sync", (lambda e, x=x, c0=c0, n=n: e.dma_start(out=x[:, :, :n], in_=xiv[:, :, c0:c0 + n])),
                 writes=[b_xt[slot]], dma=True)
            P.op("sync", (lambda e, o=o, c0=c0, n=n: e.dma_start(out=o[:, :, :n], in_=ov[:, :, c0:c0 + n])),
                 writes=[b_ot[slot]], dma=True)
            for d in range(KC):
                j = d % 2
                for k in range(KC):
                    P.op("tensor", (lambda e, j=j, k=k, d=d, o=o, n=n: e.matmul(
                        yps[j][:, :n], lhsT=w[:, k, d * 128:(d + 1) * 128], rhs=o[:, k, :n],
                        start=(k == 0), stop=(k == KC - 1))),
                        reads=[b_w, b_ot[slot]], writes=[b_y[j]])
                P.op("vector", (lambda e, x=x, j=j, d=d, n=n, s=s: e.scalar_tensor_tensor(
                    out=x[:, d, :n], in0=yps[j][:, :n], scalar=vec[:, s, d:d + 1], in1=x[:, d, :n],
                    op0=ALU.mult, op1=ALU.add)),
                    reads=[b_y[j], b_vec, b_xt[slot]], writes=[b_xt[slot]])
                P.op("gpsimd", (lambda e, x=x, d=d, n=n, s=s: e.tensor_scalar(
                    out=x[:, d, :n], in0=x[:, d, :n], scalar1=1.0, scalar2=gb[:, s, d:d + 1],
                    op0=ALU.mult, op1=ALU.add)),
                    reads=[b_gb, b_xt[slot]], writes=[b_xt[slot]])
            P.op("sync", (lambda e, x=x, c0=c0, n=n: e.dma_start(out=xov[:, :, c0:c0 + n], in_=x[:, :, :n])),
                 reads=[b_xt[slot]], dma=True)
        P.flush()


def rope_tables(half):
    t = np.arange(half * NLAT, (half + 1) * NLAT)
    rows = (t // 64).astype(np.float32)
    cols = (t % 64).astype(np.float32)
    inv = (np.float32(10000.0) ** (-np.arange(0, 32, 2, dtype=np.float32) / np.float32(32))).astype(np.float32)
    cosT = np.zeros((128, NLAT), np.float32)
    snT = np.zeros((128, NLAT), np.float32)
    for p in range(128):
        d = p % 64
        pos = rows if d < 32 else cols
        dd = d % 32
        f = dd % 16
        ang = (pos * inv[f]).astype(np.float32)
        cosT[p] = np.cos(ang)
        snT[p] = np.sin(ang) * (-1.0 if dd < 16 else 1.0)
    return cosT, snT


NAT_HEADS = 16
NEXT_ROWS = 72
NEXT = NEXT_ROWS * 64 + NCTX
NEG = -30000.0


def nat_attn_stage(nc, P, qT, kTl, kTc, gkh, vl, vc, gvh, st_q, st_k0, st_k1, rpbR, biasI, biasE, bvH, ident_d, oT):
    NCH = NEXT // 128
    NOD = NEXT_ROWS // 2 - 1
    with ExitStack() as es:
        NT = 512
        ones64 = sb(nc, es, "n_ones", [128, 128], BF16)
        ident = sb(nc, es, "n_id", [128, 128], BF16)
        qz = [[sb(nc, es, "n_q%d%d" % (i, j), [128, NTOK], BF16) for j in range(2)] for i in range(2)]
        kc = [sb(nc, es, "n_k%d" % i, [128, NEXT], BF16) for i in range(2)]
        ve = [sb(nc, es, "n_ve%d" % i, [128, NCH, 128], BF16) for i in range(2)]
        vo = [sb(nc, es, "n_vo%d" % i, [128, NOD, 128], BF16) for i in range(2)]
        bI = [sb(nc, es, "n_bi%d" % i, [128, 4, 64], BF16) for i in range(2)]
        bE = [sb(nc, es, "n_be%d" % i, [128, 8, 6, 64], BF16) for i in range(2)]
        pTc = [sb(nc, es, "n_pc%d" % i, [128, 2, NT], BF16) for i in range(2)]
        pTr = [sb(nc, es, "n_pr%d" % i, [128, 6, 64], BF16) for i in range(3)]
        rz = sb(nc, es, "n_rz", [128, NT], F32)
        ob = [sb(nc, es, "n_ob%d" % i, [128, NT], BF16) for i in range(2)]
        sq_t = sb(nc, es, "n_stq", [128, 16], F32)
        sk0_t = sb(nc, es, "n_stk0", [128, 16], F32)
        sk1_t = sb(nc, es, "n_stk1", [128, 16], F32)
        nB = sb(nc, es, "n_nB", [128, 8], F32)
        bmax = sb(nc, es, "n_bmax", [128, 1], F32)
        rpb_t = sb(nc, es, "n_rpb", [128, 16 * 15 * 31], F32)
        bv = sb(nc, es, "n_bv", [128, KC], F32)
        spr = [pp(nc, es, "n_sr%d" % i, [128, 8, 64], F32) for i in range(2)]
        spc = [pp(nc, es, "n_sc%d" % i, [128, NT], F32) for i in range(2)]
        ops_ = [pp(nc, es, "n_o%d" % i, [128, NT], F32) for i in range(2)]
        zps = [pp(nc, es, "n_z%d" % i, [128, NT], F32) for i in range(2)]
        b_one, b_id, b_st, b_nB, b_bmax, b_rpb, b_bv, b_rz = (Buf() for _ in range(8))
        b_q = [Buf(), Buf()]
        b_k = [Buf(), Buf()]
        b_ve = [Buf(), Buf()]
        b_vo = [Buf(), Buf()]
        b_bI = [Buf(), Buf()]
        b_bE = [Buf(), Buf()]
        b_pc = [Buf(), Buf()]
        b_pr = [Buf() for _ in range(3)]
        b_ob = [Buf(), Buf()]
        b_sr = [Buf(), Buf()]
        b_sc = [Buf(), Buf()]
        b_o = [Buf(), Buf()]
        b_z = [Buf(), Buf()]

        P.op("vector", lambda e: e.memset(ones64[:], 1.0), writes=[b_one])
        for i in range(2):
            for j in range(2):
                P.op("gpsimd", (lambda e, i=i, j=j: e.memset(qz[i][j][:], 0.0)), writes=[b_q[i]])
        P.op("gpsimd", lambda e: e.dma_start(out=ident[:], in_=ident_d), writes=[b_id], dma=True)
        P.op("sync", lambda e: e.dma_start(out=sq_t[:], in_=st_q), writes=[b_st], dma=True)
        P.op("sync", lambda e: e.dma_start(out=sk0_t[:], in_=st_k0), writes=[b_st], dma=True)
        P.op("sync", lambda e: e.dma_start(out=sk1_t[:], in_=st_k1), writes=[b_st], dma=True)
        P.op("sync", lambda e: e.dma_start(out=rpb_t[:], in_=rpbR), writes=[b_rpb], dma=True)
        P.op("sync", lambda e: e.dma_start(out=bv[:], in_=bvH), writes=[b_bv], dma=True)
        P.op("vector", lambda e: e.tensor_reduce(out=bmax[:], in_=rpb_t[:], axis=mybir.AxisListType.X, op=ALU.max),
             reads=[b_rpb], writes=[b_bmax])
        P.op("vector", lambda e: e.tensor_tensor(out=sk0_t[:, 8:16], in0=sk0_t[:, 8:16], in1=sk1_t[:, 8:16],
                                                 op=ALU.max), reads=[b_st], writes=[b_st])
        P.op("vector", lambda e: e.tensor_tensor(out=nB[:], in0=sq_t[:, 0:8], in1=sk0_t[:, 8:16], op=ALU.mult),
             reads=[b_st], writes=[b_nB])
        P.op("scalar", _act(nB[:], nB[:], AF.Sqrt), reads=[b_nB], writes=[b_nB])
        P.op("vector", lambda e: e.tensor_scalar(out=bmax[:], in0=bmax[:], scalar1=0.0, scalar2=None, op0=ALU.max),
             reads=[b_bmax], writes=[b_bmax])
        P.op("vector", lambda e: e.tensor_scalar(out=nB[:], in0=nB[:], scalar1=bmax[:, 0:1], scalar2=-1.0,
                                                 op0=ALU.add, op1=ALU.mult), reads=[b_nB, b_bmax], writes=[b_nB])

        cnt = {"row": 0, "pr": 0, "tile": 0, "ctx": 0}

        def load_chunk(c):
            cs_ = c % 2
            rows = slice(c * 128, (c + 1) * 128)
            cols = slice(c * 128, (c + 1) * 128)

            def ld(q, out, in_, bb):
                P.op(q, (lambda e: e.dma_start(out=out, in_=in_)), writes=[bb], dma=True)

            for j in range(2):
                ld("sync", qz[cs_][j][j * 64:(j + 1) * 64, :], qT[c * 128 + j * 64:c * 128 + (j + 1) * 64, :], b_q[cs_])
            ld("sync", kc[cs_][:, 0:256], gkh[c * 128:(c + 1) * 128, 256:512], b_k[cs_])
            ld("sync", kc[cs_][:, 256:256 + NLAT], kTl[rows, :], b_k[cs_])
            ld("sync", kc[cs_][:, 256 + NLAT:512 + NLAT], gkh[D + c * 128:D + (c + 1) * 128, 0:256], b_k[cs_])
            ld("sync", kc[cs_][:, 512 + NLAT:], kTc[rows, :], b_k[cs_])

            def tokv(ap, r0, nch):
                return ap[r0:r0 + nch * 128, cols].rearrange("(c p) e -> p c e", p=128)
            ld("sync", ve[cs_][:, 0:2, :], tokv(gvh, 256, 2), b_ve[cs_])
            ld("sync", ve[cs_][:, 2:18, :], tokv(vl, 0, 16), b_ve[cs_])
            ld("sync", ve[cs_][:, 18:34, :], tokv(vl, 2048, 16), b_ve[cs_])
            ld("sync", ve[cs_][:, 34:36, :], tokv(gvh, 512, 2), b_ve[cs_])
            ld("sync", ve[cs_][:, 36:38, :], tokv(vc, 0, 2), b_ve[cs_])
            ld("sync", vo[cs_][:, 0:1, :], tokv(gvh, 320, 1), b_vo[cs_])
            ld("sync", vo[cs_][0:64, 1, :], gvh[448:512, cols], b_vo[cs_])
            ld("sync", vo[cs_][64:128, 1, :], vl[0:64, cols], b_vo[cs_])
            ld("sync", vo[cs_][:, 2:18, :], tokv(vl, 64, 16), b_vo[cs_])
            ld("sync", vo[cs_][:, 18:33, :], tokv(vl, 64 + 2048, 15), b_vo[cs_])
            ld("sync", vo[cs_][0:64, 33, :], vl[NLAT - 64:NLAT, cols], b_vo[cs_])
            ld("sync", vo[cs_][64:128, 33, :], gvh[512:576, cols], b_vo[cs_])
            ld("sync", vo[cs_][:, 34:35, :], tokv(gvh, 576, 1), b_vo[cs_])

        def load_head(h):
            hs = h % 2
            P.op("gpsimd", (lambda e: e.dma_start(out=bI[hs][:], in_=biasI[h])), writes=[b_bI[hs]], dma=True)
            P.op("gpsimd", (lambda e: e.dma_start(out=bE[hs][:], in_=biasE[h])), writes=[b_bE[hs]], dma=True)

        def row_plan(r_l):
            if r_l < 4:
                return [(128 * i, 0, i, ("E", r_l, i)) for i in range(6)]
            if r_l >= 60:
                return [(128 * (30 + i), 0, 30 + i, ("E", r_l - 56, i)) for i in range(6)]
            plan = []
            for i in range(4):
                er = r_l + 2 * i
                if er % 2 == 0:
                    plan.append((er * 64, 0, er // 2, ("I", i)))
                else:
                    plan.append((er * 64, 1, (er - 1) // 2, ("I", i)))
            return plan

        def do_tile(h, c0, n, s):
            cs_ = (h // 2) % 2
            hs = h % 2
            hh = h % 2
            pr = slice(hh * 64, (hh + 1) * 64)
            cc = h // 2
            tj = cnt["tile"] % 2
            cnt["tile"] += 1
            o_ps, z_ps = ops_[tj], zps[tj]
            bo, bz = b_o[tj], b_z[tj]
            xj = cnt["ctx"] % 2
            cnt["ctx"] += 1
            for ci in range(2):
                k0 = NEXT_ROWS * 64 + ci * 128
                sj = ci
                P.op("tensor", (lambda e, k0=k0, sj=sj: e.matmul(
                    spc[sj][:, :n], lhsT=kc[cs_][:, k0:k0 + 128], rhs=qz[cs_][hh][:, c0:c0 + n], start=True, stop=True)),
                    reads=[b_k[cs_], b_q[cs_]], writes=[b_sc[sj]])
                P.op("scalar", _act(pTc[xj][:, ci, :n], spc[sj][:, :n], AF.Exp, bias=nB[:, cc:cc + 1], scale=1.0),
                     reads=[b_sc[sj], b_nB], writes=[b_pc[xj]])

            def ctx_pv(q0, nq, first_is_start, last):
                for ci in range(2):
                    P.op("tensor", (lambda e, ci=ci: e.matmul(
                        o_ps[:, q0:q0 + nq], lhsT=ve[cs_][:, NCH - 2 + ci, :],
                        rhs=pTc[xj][:, ci, q0:q0 + nq], start=(first_is_start and ci == 0), stop=(last and ci == 1))),
                        reads=[b_ve[cs_], b_pc[xj]], writes=[bo])
                    P.op("tensor", (lambda e, ci=ci: e.matmul(
                        z_ps[:, q0:q0 + nq], lhsT=ones64[:], rhs=pTc[xj][:, ci, q0:q0 + nq],
                        start=(first_is_start and ci == 0), stop=(last and ci == 1))),
                        reads=[b_one, b_pc[xj]], writes=[bz])

            if s == 1:
                ctx_pv(0, n, True, True)
            else:
                rows = list(range(c0 // 64, c0 // 64 + n // 64))

                def scores(r_l):
                    rj = cnt["row"] % 2
                    cnt["row"] += 1
                    plan = row_plan(r_l)
                    q0 = r_l * 64
                    for i, (k0, vk, vc, bt) in enumerate(plan):
                        P.op("tensor", (lambda e, i=i, k0=k0: e.matmul(
                            spr[rj][:, i, :], lhsT=kc[cs_][:, k0:k0 + 128], rhs=qz[cs_][hh][:, q0:q0 + 64],
                            start=True, stop=False)),
                            reads=[b_k[cs_], b_q[cs_]], writes=[b_sr[rj]])
                        if bt[0] == "I":
                            btile, bb = bI[hs][:, bt[1], :], b_bI[hs]
                        else:
                            btile, bb = bE[hs][:, bt[1], bt[2], :], b_bE[hs]
                        P.op("tensor", (lambda e, i=i, btile=btile: e.matmul(
                            spr[rj][:, i, :], lhsT=ident[:], rhs=btile, start=False, stop=True)),
                            reads=[b_id, bb], writes=[b_sr[rj]])
                    return rj, plan

                def finish_row(r_l, rj, plan, rr):
                    np_ = len(plan)
                    pj = cnt["pr"] % 3
                    cnt["pr"] += 1
                    P.op("scalar", _act(pTr[pj][:, :np_, :], spr[rj][:, :np_, :], AF.Exp, bias=nB[:, cc:cc + 1],
                                        scale=1.0),
                         reads=[b_sr[rj], b_nB], writes=[b_pr[pj]])
                    q0 = rr * 64
                    for i, (k0, vk, vc, bt) in enumerate(plan):
                        vt, bvb = (ve[cs_], b_ve[cs_]) if vk == 0 else (vo[cs_], b_vo[cs_])
                        P.op("tensor", (lambda e, i=i, vt=vt, vc=vc: e.matmul(
                            o_ps[:, q0:q0 + 64], lhsT=vt[:, vc, :], rhs=pTr[pj][:, i, :],
                            start=(i == 0), stop=False)),
                            reads=[bvb, b_pr[pj]], writes=[bo])
                        P.op("tensor", (lambda e, i=i: e.matmul(
                            z_ps[:, q0:q0 + 64], lhsT=ones64[:], rhs=pTr[pj][:, i, :], start=(i == 0), stop=False)),
                            reads=[b_one, b_pr[pj]], writes=[bz])
                    ctx_pv(q0, 64, False, True)

                prev = None
                for rr, r_l in enumerate(rows):
                    rj, plan = scores(r_l)
                    if prev is not None:
                        finish_row(*prev)
                    prev = (r_l, rj, plan, rr)
                finish_row(*prev)
            oj = tj
            P.op("vector", (lambda e: e.reciprocal(out=rz[pr, :n], in_=z_ps[pr, :n])), reads=[bz], writes=[b_rz])
            P.op("vector", (lambda e: e.tensor_tensor(out=rz[pr, :n], in0=o_ps[pr, :n], in1=rz[pr, :n], op=ALU.mult)),
                 reads=[bo, b_rz], writes=[b_rz])
            P.op("gpsimd", (lambda e: e.tensor_scalar(out=ob[oj][pr, :n], in0=rz[pr, :n], scalar1=1.0,
                                                      scalar2=bv[pr, cc:cc + 1], op0=ALU.mult, op1=ALU.add)),
                 reads=[b_rz, b_bv], writes=[b_ob[oj]])
            P.op("sync", (lambda e: e.dma_start(out=oT[h * 64:(h + 1) * 64, c0:c0 + n], in_=ob[oj][pr, :n])),
                 reads=[b_ob[oj]], dma=True)

        tiles = lat_ctx_tiles()
        load_chunk(0)
        load_head(0)
        for h in range(NAT_HEADS):
            if h % 2 == 0 and h // 2 + 1 < KC:
                load_chunk(h // 2 + 1)
            if h + 1 < NAT_HEADS:
                load_head(h + 1)
            for (c0, n, s) in tiles:
                do_tile(h, c0, n, s)
        P.flush()


def nat_bias_tables(rpb, half):
    c = np.arange(64)
    cs = np.clip(c - 8, 0, 48)
    kcol = np.arange(64)
    inwin = (kcol[:, None] >= cs[None, :]) & (kcol[:, None] < cs[None, :] + 16)
    off = np.clip(kcol[:, None] - c[None, :] + 15, 0, 30)
    T = np.where(inwin[None, None], rpb[:, :, off], np.float32(NEG)).astype(np.float32)
    bI = np.empty((16, 128, 4, 64), np.float32)
    for i in range(4):
        for rho in range(2):
            bI[:, rho * 64:(rho + 1) * 64, i, :] = T[:, 2 * i + rho + 3]
    bE = np.full((16, 128, 8, 6, 64), NEG, np.float32)
    for e in range(8):
        r_l = e if e < 4 else 56 + e
        base = 0 if e < 4 else 60
        r = half * 64 + r_l
        rs = min(max(r - 4, 0), 120)
        for i in range(6):
            for rho in range(2):
                gk = half * 64 - 4 + base + 2 * i + rho
                if rs <= gk <= rs + 7:
                    bE[:, rho * 64:(rho + 1) * 64, e, i, :] = T[:, gk - r + 7]
    return bI, bE


def nat_ext_layout(arrs, c, axis):
    half = c % 2
    me, partner = arrs[c], arrs[c ^ 1]
    def tok(a, lo, hi):
        return a[:, lo:hi] if axis == 1 else a[lo:hi]
    z = np.zeros_like(tok(me, 0, 256))
    top = z if half == 0 else tok(partner, NLAT - 256, NLAT)
    bot = tok(partner, 0, 256) if half == 0 else z
    return np.ascontiguousarray(np.concatenate([top, tok(me, 0, NLAT), bot, tok(me, NLAT, NTOK)], axis=axis))


class Launch:
    def __init__(self):
        self.nc = bass.Bass("TRN2", target_bir_lowering=False)
        self.ins = [dict() for _ in range(NCORES)]
        self.outs = []

    def din(self, name, per_core, dt=F32):
        a0 = per_core[0]
        t = self.nc.dram_tensor(name, list(a0.shape), dt, kind="ExternalInput").ap()
        for c in range(NCORES):
            self.ins[c][name] = np.ascontiguousarray(per_core[c])
        return t

    def dout(self, name, shape, dt=F32):
        self.outs.append(name)
        return self.nc.dram_tensor(name, list(shape), dt, kind="ExternalOutput").ap()

    def dint(self, name, shape, dt=F32):
        return self.nc.dram_tensor(name, list(shape), dt).ap()

    def run(self):
        res = run_bass_kernel_spmd(self.nc, self.ins, core_ids=list(range(NCORES)))
        return res.results


def same(a):
    return [a] * NCORES


LAT_TILES = [(c0, 512, 0) for c0 in range(0, NLAT, 512)]


GROUPS = [[0, 1], [2, 3], [4, 5], [6, 7]]


def all_gather(P, src, dst, reads, writes):
    P.cc(lambda e: e.collective_compute("AllGather", ALU.bypass, replica_groups=GROUPS,
                                        ins=[src.opt()], outs=[dst.opt()]),
         reads=reads, writes=writes)


def exchange_edges(nc, P, L, tag, xsrc):
    e16 = L.dint("e16" + tag, [D, 16])
    g16 = L.dint("g16" + tag, [2 * D, 16])
    b_e, b_g = Buf(), Buf()
    P.op("sync", lambda e: e.dma_start(out=e16[:, 0:8], in_=xsrc[:, 0:8]), writes=[b_e], dma=True)
    P.op("sync", lambda e: e.dma_start(out=e16[:, 8:16], in_=xsrc[:, NLAT - 8:NLAT]), writes=[b_e], dma=True)
    all_gather(P, e16, g16, [b_e], [b_g])
    P.flush()
    return g16


def kernel(x, c, ctx, c_ctx, ada_w, ada_b, norm_g, ffn_w_in, ffn_w_out, pool_w, pool_scale,
           diff_w_qkv, diff_lam, diff_subln_g, diff_w_o, nat_w_qkv, nat_b_qkv, nat_rpb,
           nat_w_o, nat_b_o, final_g):
    f32 = lambda a: np.asarray(a, np.float32)
    x, c, ctx, c_ctx = f32(x), f32(c), f32(ctx), f32(c_ctx)
    ada_w, ada_b, norm_g = f32(ada_w), f32(ada_b), f32(norm_g)
    ffn_w_in, ffn_w_out = f32(ffn_w_in), f32(ffn_w_out)
    halves = [cc % 2 for cc in range(NCORES)]
    bats = [cc // 2 for cc in range(NCORES)]
    all_tiles = lat_ctx_tiles()

    xs = [np.concatenate([x[b, h * NLAT:(h + 1) * NLAT], ctx[b]], 0).T for b, h in zip(bats, halves)]
    cT = [fm(np.stack([c[b], c_ctx], 0)).transpose(0, 2, 1) for b in bats]
    ada_bT = np.ascontiguousarray(ada_b.reshape(DEPTH, 72, 128).transpose(2, 0, 1))
    gT = fm(norm_g)
    hmask = []
    for h in halves:
        m = np.zeros((128, 2), np.float32)
        m[:, 0] = 1.0 if h == 1 else 0.0
        m[:, 1] = 1.0 if h == 0 else 0.0
        hmask.append(m)
    edges = [pool_edge_tables(h) for h in halves]
    cos_sn = [rope_tables(h) for h in halves]
    bqkv = f32(nat_b_qkv[0])
    bq_fm = fm(bqkv[:2 * D].reshape(2, D)).reshape(128, 16)
    tabs = [nat_bias_tables(f32(nat_rpb[0]), h) for h in halves]
    lam_init = 0.8 - 0.6 * math.exp(-0.3 * 1)

    L = Launch()
    nc = L.nc
    i_cT = L.din("cT", cT)
    i_ada_w = L.din("ada_w", same(ada_w))
    i_ada_bT = L.din("ada_bT", same(ada_bT))
    i_gT = L.din("gT", same(gT))
    i_x = L.din("xin", xs)
    i_win = [[L.din("w_in%d%d" % (l, u), same(ffn_w_in[l, u])) for u in range(2)] for l in range(DEPTH)]
    i_wout = [[L.din("w_out%d%d" % (l, u), same(ffn_w_out[l, u])) for u in range(2)] for l in range(DEPTH)]
    i_pw = [L.din("pool_w%d" % j, same(f32(pool_w[j]))) for j in range(2)]
    i_psc = [L.din("pscT%d" % j, same(fm(pool_scale[j]))) for j in range(2)]
    i_edge = L.din("edge", edges)
    i_hmask = L.din("hmask", hmask)
    i_dqkv = L.din("d_wqkv", same(f32(diff_w_qkv[0])))
    i_zero16 = L.din("zero16", same(np.zeros((128, 16), np.float32)))
    i_zero8 = L.din("zero8", same(np.zeros((128, KC), np.float32)))
    i_cos = L.din("cosT", [t[0] for t in cos_sn])
    i_sn = L.din("snT", [t[1] for t in cos_sn])
    i_lam = L.din("lamR", same(np.broadcast_to(f32(diff_lam[0]), (128, 4, 64))))
    i_subln = L.din("subln", same(f32(diff_subln_g[0]).reshape(128, 1)))
    i_dwo = L.din("d_wo", same(f32(diff_w_o[0])))
    i_nqkv = L.din("n_wqkv", same(f32(nat_w_qkv[0])))
    i_nbq = L.din("n_bq", same(bq_fm))
    i_rpbR = L.din("rpbR", same(np.broadcast_to(f32(nat_rpb[0]).reshape(1, -1), (128, 7440))))
    i_bI = L.din("bI", [t[0] for t in tabs])
    i_bE = L.din("bE", [t[1] for t in tabs])
    i_bvH = L.din("bvH", same(fm(bqkv[2 * D:])))
    i_ident = L.din("ident", same(np.eye(128, dtype=np.float32)))
    i_nwo = L.din("n_wo", same(f32(nat_w_o[0])))
    i_nbo = L.din("n_bo", same(fm(nat_b_o[0])))
    i_gf = L.din("gfT", same(fm(final_g)))
    yo = L.dout("y", [D, NLAT])

    vec = L.dint("vec", [128, DEPTH, 9, 2, KC])
    X = [L.dint("xs%d" % i, [D, NTOK]) for i in range(12)]

    with ExitStack() as es:
        P = Prog(nc, es)
        mods_stage(nc, P, i_cT, i_ada_w, i_ada_bT, i_gT, vec)
        ffn_stage(nc, P, i_x, X[0], i_win[0][0], i_wout[0][0], vec, 0, 0, all_tiles)
        g16a = exchange_edges(nc, P, L, "a", X[0])
        pool_stage(nc, P, X[0], g16a, X[1], i_pw[0], i_psc[0], i_edge, i_hmask, vec, 0, True)
        ffn_stage(nc, P, X[1], X[2], i_win[0][1], i_wout[0][1], vec, 0, 2, all_tiles)
        ffn_stage(nc, P, X[2], X[3], i_win[1][0], i_wout[1][0], vec, 1, 0, all_tiles)
        qT1 = L.dint("qT1", [D, NTOK], BF16)
        NLT = NLAT // 512
        kTl1 = [L.dint("kTl1_%d" % t, [D, 512], BF16) for t in range(NLT)]
        kTc1 = L.dint("kTc1", [D, NCTX], BF16)
        vl1 = [L.dint("vl1_%d" % t, [512, D], BF16) for t in range(NLT)]
        vc1 = L.dint("vc1", [NCTX, D], BF16)
        st1 = L.dint("st1", [128, 16])
        gk = [L.dint("gk%d" % t, [2 * D, 512], BF16) for t in range(NLT)]
        gv = [L.dint("gv%d" % t, [2 * 512, D], BF16) for t in range(NLT)]
        gst1 = L.dint("gst1", [256, 16])
        qkv_stage(nc, P, X[3], i_dqkv, i_zero16, vec, 1, i_cos, i_sn, 1.0, 1, qT1, kTl1, kTc1, vl1, vc1, st1,
                  all_tiles, gather=(gk, gv))
        all_gather(P, st1, gst1, [], [Buf()])
        P.flush()
        k_pieces, v_pieces = [], []
        for t in range(NLT):
            k_pieces += [(gk[t][0:D, :], 512), (gk[t][D:2 * D, :], 512)]
            v_pieces += [(gv[t], 1024)]
        k_pieces.append((kTc1, NCTX))
        v_pieces.append((vc1, NCTX))
        oT1 = L.dint("oT1", [D, NTOK], BF16)
        diff_attn_stage(nc, P, qT1, k_pieces, v_pieces, st1, gst1[0:128, :], gst1[128:256, :],
                        i_lam, i_subln, lam_init, oT1, 2 * NLAT)
        proj_stage(nc, P, oT1, i_dwo, i_zero8, X[3], X[4], vec, 1, all_tiles)
        ffn_stage(nc, P, X[4], X[5], i_win[1][1], i_wout[1][1], vec, 1, 2, all_tiles)
        ffn_stage(nc, P, X[5], X[6], i_win[2][0], i_wout[2][0], vec, 2, 0, all_tiles)
        qT2 = L.dint("qT2", [D, NTOK], BF16)
        kTl2 = L.dint("kTl2", [D, NLAT], BF16)
        kTc2 = L.dint("kTc2", [D, NCTX], BF16)
        vl2 = L.dint("vl2", [NLAT, D], BF16)
        vc2 = L.dint("vc2", [NCTX, D], BF16)
        st2 = L.dint("st2", [128, 16])
        qkv_stage(nc, P, X[6], i_nqkv, i_nbq, vec, 2, None, None, 64 ** -0.5, 1, qT2, kTl2, kTc2, vl2, vc2, st2,
                  all_tiles)
        kh = L.dint("khalo", [D, 512], BF16)
        vh = L.dint("vhalo", [512, D], BF16)
        gkh = L.dint("gkh", [2 * D, 512], BF16)
        gvh = L.dint("gvh", [1024, D], BF16)
        gst2 = L.dint("gst2", [256, 16])
        b_kh, b_vh = Buf(), Buf()
        P.op("sync", lambda e: e.dma_start(out=kh[:, 0:256], in_=kTl2[:, 0:256]), writes=[b_kh], dma=True)
        P.op("sync", lambda e: e.dma_start(out=kh[:, 256:512], in_=kTl2[:, NLAT - 256:NLAT]), writes=[b_kh], dma=True)
        P.op("sync", lambda e: e.dma_start(out=vh[0:256, :], in_=vl2[0:256, :]), writes=[b_vh], dma=True)
        P.op("sync", lambda e: e.dma_start(out=vh[256:512, :], in_=vl2[NLAT - 256:NLAT, :]), writes=[b_vh], dma=True)
        all_gather(P, kh, gkh, [b_kh], [Buf()])
        all_gather(P, vh, gvh, [b_vh], [Buf()])
        all_gather(P, st2, gst2, [], [Buf()])
        P.flush()
        oT2 = L.dint("oT2", [D, NTOK], BF16)
        nat_attn_stage(nc, P, qT2, kTl2, kTc2, gkh, vl2, vc2, gvh, st2, gst2[0:128, :], gst2[128:256, :],
                       i_rpbR, i_bI, i_bE, i_bvH, i_ident, oT2)
        proj_stage(nc, P, oT2, i_nwo, i_nbo, X[6], X[7], vec, 2, LAT_TILES)
        ffn_stage(nc, P, X[7], X[8], i_win[2][1], i_wout[2][1], vec, 2, 2, LAT_TILES)
        ffn_stage(nc, P, X[8], X[9], i_win[3][0], i_wout[3][0], vec, 3, 0, LAT_TILES)
        g16b = exchange_edges(nc, P, L, "b", X[9])
        pool_stage(nc, P, X[9], g16b, X[10], i_pw[1], i_psc[1], i_edge, i_hmask, vec, 3, False)
        ffn_stage(nc, P, X[10], yo, i_win[3][1], i_wout[3][1], vec, 3, 2, LAT_TILES, final_g=i_gf)
    res = L.run()

    out = np.empty((4, 2 * NLAT, D), np.float32)
    for cc in range(NCORES):
        out[bats[cc], halves[cc] * NLAT:(halves[cc] + 1) * NLAT, :] = res[cc]["y"].T
    return out
```
